# Optimizing a Trainium2 kernel written in Bass

```python
import jax, jax.numpy as jnp
from jax import lax
import numpy as np

D_MODEL = 1024
BATCH = 16
SEQ = 4096
DEPTH = 1
DEC_BATCH = 4
DEC_SEQ = 8192
PAST_LEN = 128

HEAD_DIM = 64
HEADS_PER_GROUP = 8
ATTN_PATTERNS = ((128, 1), (512, 4), (2048, 16))
N_GROUPS = len(ATTN_PATTERNS)
N_ATTN_HEADS = N_GROUPS * HEADS_PER_GROUP
ATTN_WIDTH = N_ATTN_HEADS * HEAD_DIM
ATTN_OUT = HEADS_PER_GROUP * HEAD_DIM
SGU_WIDTH = D_MODEL
CHUNK = 128
SGU_GROUPS = 8
SGU_GROUP_DIM = SGU_WIDTH // SGU_GROUPS
D_FF = 4 * D_MODEL
IN_COLS = 3 * ATTN_WIDTH + 2 * SGU_WIDTH + 2 * D_MODEL
EPS = 1e-6
MASK_VALUE = -1e30

kernel_name = "gated_dilated_attn_sgu_encoder"


def rms_norm(x, g):
    xf = x.astype(jnp.float32)
    y = xf * lax.rsqrt(jnp.mean(xf * xf, axis=-1, keepdims=True) + EPS)
    return (y * g.astype(jnp.float32)).astype(x.dtype)


def alibi_slopes(n):
    return jnp.asarray(2.0 ** (-8.0 * np.arange(1, n + 1) / n), dtype=jnp.float32)


def dilated_window_attention(q, k, v, window, dilation, slopes):
    B, S, H, hd = q.shape
    d = dilation
    n = window // (2 * d)
    L = S // d
    nb = -(-L // n)
    Lp = nb * n

    def by_residue(t):
        return t.reshape(B, L, d, H, hd).transpose(0, 2, 1, 3, 4)

    qr, kr, vr = by_residue(q), by_residue(k), by_residue(v)
    qb = jnp.pad(qr, ((0, 0), (0, 0), (0, Lp - L), (0, 0), (0, 0))).reshape(B, d, nb, n, H, hd)
    kv_pad = ((0, 0), (0, 0), (n, Lp - L + n), (0, 0), (0, 0))

    def band(t):
        tb = jnp.pad(t, kv_pad).reshape(B, d, nb + 2, n, H, hd)
        return jnp.concatenate([tb[:, :, :nb], tb[:, :, 1:nb + 1], tb[:, :, 2:]], axis=3)

    kb, vb = band(kr), band(vr)
    s = jnp.einsum('bciqhe,bcikhe->bcihqk', qb, kb,
                   preferred_element_type=jnp.float32) * (hd ** -0.5)
    rel = jnp.arange(3 * n)[None, :] - n - jnp.arange(n)[:, None]
    m_k = jnp.arange(nb)[:, None] * n + jnp.arange(3 * n)[None, :] - n
    valid = (jnp.abs(rel) <= n)[None] & ((m_k >= 0) & (m_k < L))[:, None, :]
    dist = (d * jnp.abs(rel)).astype(jnp.float32)
    s = s - slopes[:, None, None] * dist[None]
    s = jnp.where(valid[None, None, :, None], s, MASK_VALUE)
    row_max = jnp.max(s, axis=-1)
    p = jnp.exp(s - row_max[..., None])
    denom = jnp.sum(p, axis=-1)
    o = jnp.einsum('bcihqk,bcikhe->bciqhe', p, vb.astype(jnp.float32))
    o = o / jnp.swapaxes(denom, -1, -2)[..., None]

    def from_residue(t):
        t = t.reshape((B, d, Lp) + t.shape[4:])[:, :, :L]
        t = jnp.moveaxis(t, 1, 2)
        return t.reshape((B, S) + t.shape[3:])

    return (from_residue(o),
            from_residue(jnp.swapaxes(row_max, -1, -2)),
            from_residue(jnp.swapaxes(denom, -1, -2)))


def attention_mixer(q, k, v):
    slopes = alibi_slopes(N_ATTN_HEADS).reshape(N_GROUPS, HEADS_PER_GROUP)
    outs, maxs, dens = [], [], []
    for g, (window, dilation) in enumerate(ATTN_PATTERNS):
        o, mx, dn = dilated_window_attention(q[:, :, g], k[:, :, g], v[:, :, g], window, dilation, slopes[g])
        outs.append(o); maxs.append(mx); dens.append(dn)
    outs = jnp.stack(outs)
    maxs = jnp.stack(maxs)
    dens = jnp.stack(dens)
    w = dens * jnp.exp(maxs - jnp.max(maxs, axis=0, keepdims=True))
    return jnp.sum(w[..., None] * outs, axis=0) / jnp.sum(w, axis=0)[..., None]


def spatial_gating(u, v, w_s, b_s, g_sgu):
    B, S, _ = u.shape
    v = rms_norm(v, g_sgu)
    vc = v.reshape(B, S // CHUNK, CHUNK, SGU_GROUPS, SGU_GROUP_DIM)
    mixed = jnp.einsum('gts,bcsge->bctge', w_s, vc) + b_s.T[None, None, :, :, None]
    return u * mixed.reshape(B, S, SGU_WIDTH)


def encoder_layer(x, g_mix, w_in, w_s, b_s, g_sgu, w_branch_a, w_branch_b, w_out, g_mlp, w_up, w_down):
    B, S, _ = x.shape
    h = rms_norm(x, g_mix)
    proj = h @ w_in
    cuts = np.cumsum([ATTN_WIDTH, ATTN_WIDTH, ATTN_WIDTH, 2 * SGU_WIDTH, D_MODEL]).tolist()
    q, k, v, z, ga, gb = jnp.split(proj, cuts, axis=-1)
    hshape = (B, S, N_GROUPS, HEADS_PER_GROUP, HEAD_DIM)
    attn = attention_mixer(q.reshape(hshape), k.reshape(hshape), v.reshape(hshape))
    attn = attn.reshape(B, S, ATTN_OUT).astype(x.dtype)
    z = jax.nn.gelu(z, approximate=False)
    u, vs = jnp.split(z, 2, axis=-1)
    sgu = spatial_gating(u, vs, w_s, b_s, g_sgu)
    merged = jax.nn.sigmoid(ga) * (attn @ w_branch_a) + jax.nn.sigmoid(gb) * (sgu @ w_branch_b)
    x = x + merged @ w_out
    h2 = rms_norm(x, g_mlp)
    x = x + jnp.square(jax.nn.relu(h2 @ w_up)) @ w_down
    return x


def encoder_forward(x, g_mix, w_in, w_s, b_s, g_sgu, w_branch_a, w_branch_b, w_out, g_mlp, w_up, w_down, g_final):
    for l in range(DEPTH):
        x = encoder_layer(x, g_mix[l], w_in[l], w_s[l], b_s[l], g_sgu[l], w_branch_a[l], w_branch_b[l],
                          w_out[l], g_mlp[l], w_up[l], w_down[l])
    return rms_norm(x, g_final)


def setup_inputs(seed: int = 0) -> dict:
    key = jax.random.key(seed)
    ks = jax.random.split(key, 16)
    f32 = jnp.float32

    def normal(k, shape, scale):
        return jax.random.normal(k, shape, f32) * scale

    return {
        "x_prompt": normal(ks[0], (BATCH, SEQ, D_MODEL), 1.0),
        "x_sample": normal(ks[1], (DEC_BATCH, DEC_SEQ, D_MODEL), 1.0),
        "g_mix": 1.0 + normal(ks[2], (DEPTH, D_MODEL), 0.02),
        "w_in": normal(ks[3], (DEPTH, D_MODEL, IN_COLS), D_MODEL ** -0.5),
        "w_s": normal(ks[4], (DEPTH, SGU_GROUPS, CHUNK, CHUNK), CHUNK ** -0.5),
        "b_s": 1.0 + normal(ks[5], (DEPTH, SGU_GROUPS, CHUNK), 0.02),
        "g_sgu": 1.0 + normal(ks[6], (DEPTH, SGU_WIDTH), 0.02),
        "w_branch_a": normal(ks[7], (DEPTH, ATTN_OUT, D_MODEL), ATTN_OUT ** -0.5),
        "w_branch_b": normal(ks[8], (DEPTH, SGU_WIDTH, D_MODEL), SGU_WIDTH ** -0.5),
        "w_out": normal(ks[9], (DEPTH, D_MODEL, D_MODEL), D_MODEL ** -0.5),
        "g_mlp": 1.0 + normal(ks[10], (DEPTH, D_MODEL), 0.02),
        "w_up": normal(ks[11], (DEPTH, D_MODEL, D_FF), D_MODEL ** -0.5),
        "w_down": normal(ks[12], (DEPTH, D_FF, D_MODEL), D_FF ** -0.5),
        "g_final": 1.0 + normal(ks[13], (D_MODEL,), 0.02),
    }


def reference(x_prompt, x_sample, g_mix, w_in, w_s, b_s, g_sgu, w_branch_a, w_branch_b, w_out,
              g_mlp, w_up, w_down, g_final):
    y_prompt = encoder_forward(x_prompt, g_mix, w_in, w_s, b_s, g_sgu, w_branch_a, w_branch_b, w_out,
                               g_mlp, w_up, w_down, g_final)
    y_sample = encoder_forward(x_sample, g_mix, w_in, w_s, b_s, g_sgu, w_branch_a, w_branch_b, w_out,
                               g_mlp, w_up, w_down, g_final)
    return (y_prompt, y_sample)
```

```python
import contextlib
import numpy as np
import ml_dtypes
import concourse.bass as bass
import concourse.mybir as mybir
from concourse.bass_utils import run_bass_kernel_spmd

F32, BF16 = mybir.dt.float32, mybir.dt.bfloat16
AF = mybir.ActivationFunctionType
ALU = mybir.AluOpType

DM = 1024
DIL = (1, 4, 16)
PADK = 2048
NEGV = -30000.0
EPS = 1e-6


class Sem:
    def __init__(self, h):
        self.h = h
        self.n = 0


class Slot:
    def __init__(self):
        self.w = {}
        self.r = {}


class Prog:
    ENGS = ("sync", "scalar", "vector", "gpsimd", "tensor")

    def __init__(self, nc, stack):
        self.nc, self.stack = nc, stack
        self.q = {e: [] for e in self.ENGS}
        self.waited = {e: {} for e in self.ENGS}
        self.psem = {}
        self.nsem = 0
        self.allsems = []
        self.new_phase()

    def sem(self, name):
        self.nsem += 1
        s = Sem(self.stack.enter_context(self.nc.semaphore(f"{name}{self.nsem}")))
        self.allsems.append(s)
        return s

    def new_phase(self):
        for e in ("scalar", "vector", "tensor", "gpsimd"):
            self.psem[e] = self.sem("p" + e[0])

    def _waits(self, eng, reads, writes, deps):
        own = self.psem.get(eng)
        need = {}
        for sl in reads:
            for k, v in sl.w.items():
                if k is own and eng == "tensor":
                    continue
                need[k] = max(need.get(k, 0), v)
        for sl in writes:
            for k, v in sl.r.items():
                if k is not own:
                    need[k] = max(need.get(k, 0), v)
            for k, v in sl.w.items():
                if k is not own:
                    need[k] = max(need.get(k, 0), v)
        for (k, v) in deps:
            if k is not own:
                need[k] = max(need.get(k, 0), v)
        waits = []
        for k, v in need.items():
            if self.waited[eng].get(k, 0) < v:
                waits.append((k, v))
                self.waited[eng][k] = v
        return waits

    def op(self, eng, fn, reads=(), writes=(), deps=(), selfdeps=()):
        waits = self._waits(eng, reads, writes, deps)
        own = self.psem[eng]
        for (k, v) in selfdeps:
            assert k is own
            if self.waited[eng].get(k, 0) < v:
                waits.append((k, v))
                self.waited[eng][k] = v
        own.n += 1
        self.q[eng].append((fn, waits, own, 1))
        for sl in reads:
            sl.r[own] = own.n
        for sl in writes:
            sl.w[own] = own.n
        return (own, own.n)

    def dma(self, eng, fn, dsem, reads=(), writes=(), deps=(), nowaw=True):
        if nowaw:
            need_w = []
            waits = self._waits(eng, reads, (), deps)
            need = {}
            for sl in writes:
                for k, v in sl.r.items():
                    need[k] = max(need.get(k, 0), v)
            for k, v in need.items():
                if self.waited[eng].get(k, 0) < v:
                    waits.append((k, v))
                    self.waited[eng][k] = v
        else:
            waits = self._waits(eng, reads, writes, deps)
        dsem.n += 16
        self.q[eng].append((fn, waits, dsem, 16))
        for sl in reads:
            sl.r[dsem] = dsem.n
        for sl in writes:
            sl.w[dsem] = dsem.n
        return (dsem, dsem.n)

    def barrier(self):
        for e in self.ENGS:
            waits = []
            own = self.psem.get(e)
            for s in self.allsems:
                if s is own or s.n == 0:
                    continue
                if self.waited[e].get(s, 0) < s.n:
                    waits.append((s, s.n))
                    self.waited[e][s] = s.n
            if waits:
                self.q[e].append((None, waits, None, 0))

    def emit(self):
        with self.nc.Block() as block:
            for e in self.ENGS:
                items = self.q[e]

                def body(eng, items=items):
                    for fn, waits, sem, inc in items:
                        for (k, v) in waits:
                            eng.wait_ge(k.h, v)
                        if fn is not None:
                            ins = fn(eng)
                            ins.then_inc(sem.h, inc)

                getattr(block, e)(body)


class Arena:
    def __init__(self, t, n):
        self.t, self.n, self.o = t, n, 0

    def reset(self):
        self.o = 0

    def take(self, n):
        assert self.o + n <= self.n, (self.o, n, self.n)
        ap = self.t[:, self.o:self.o + n]
        self.o += n
        return ap


def ssl(start, n, step):
    return slice(start, start + (n - 1) * step + 1, step)


DEBUG = False


def build(NSEG):
    NT = NSEG * 4096
    NS = NT // 2048
    KW = PADK + NT + PADK
    NAg = [NT // (128 * d) + 2 for d in DIL]

    nc = bass.Bass("TRN2", target_bir_lowering=False)
    din = lambda n, s, dt=F32: nc.dram_tensor(n, s, dt, kind="ExternalInput").ap()
    x_d = din("x", [NT, DM])
    w_in_d = din("w_in", [DM, 8704])
    gmix_d = din("g_mix_b", [128, DM])
    gsgu_d = din("g_sgu_b", [128, DM])
    gmlp_d = din("g_mlp_b", [128, DM])
    gfin_d = din("g_fin_b", [128, DM])
    wsT_d = din("w_sT", [128, 8 * 128])
    bs_d = din("b_s", [1, 1024])
    wa_d = din("w_a", [512, DM])
    wb_d = din("w_b", [DM, DM])
    wo_d = din("w_o", [DM, DM])
    wup_d = din("w_up", [DM, 4096])
    wdn_d = din("w_dn", [4096, DM])
    id_d = din("ident", [128, 128])
    tab_d = din("tab", [128, 24 * 256], BF16)
    bnd_d = din("bnd", [128, 24 * 128], BF16)
    y_d = nc.dram_tensor("y", [NT, DM], F32, kind="ExternalOutput").ap()
    scr = lambda n, s, dt: nc.dram_tensor(n, s, dt, kind=("ExternalOutput" if DEBUG else "Internal")).ap()
    hT_s = scr("hT_s", [DM, NT], BF16)
    Q_s = scr("Q_s", [1536, NT], BF16)
    K_s = scr("K_s", [1536, KW], BF16)
    V_s = [scr(f"V_s{g}", [4, DIL[g], 128, NAg[g], 256], BF16) for g in range(3)]
    at_s = scr("at_s", [512, NT], BF16)
    x1_s = scr("x1_s", [NT, DM], F32)

    NB16 = 83456
    NF32 = 11008
    stack = contextlib.ExitStack()
    with stack:
        ab_t = stack.enter_context(nc.sbuf_tensor("arena_b", [128, NB16], BF16))
        af_t = stack.enter_context(nc.sbuf_tensor("arena_f", [128, NF32], F32))
        psb = [stack.enter_context(nc.psum_tensor(f"ps{i}", [128, 512], F32)) for i in range(8)]
        AB = Arena(ab_t, NB16)
        AFa = Arena(af_t, NF32)
        P = Prog(nc, stack)
        PS = [Slot() for _ in range(8)]
        psctr = [0]

        def nbank(lo=0, hi=8):
            i = lo + psctr[0] % (hi - lo)
            psctr[0] += 1
            return i

        ident = AFa.take(128)
        S_const = Slot()
        P.dma("sync", lambda e: e.dma_start(out=ident, in_=id_d), P.sem("dc"), writes=[S_const])
        f32_base = AFa.o

        def cast_load(dst, src, sl, dsem, maxcols=2048):
            ncol = src.shape[-1]
            c0 = 0
            while c0 < ncol:
                c1 = min(ncol, c0 + maxcols)
                P.dma("gpsimd", lambda e, a=dst[:, c0:c1], b=src[:, c0:c1]: e.dma_start(out=a, in_=b),
                      dsem, writes=[sl])
                c0 = c1

        def norm_subtile(st, u, src_rows, gb, S_gb, hT_dst, S_hT, keep_xt=False):
            sl = u % st["nxt"]
            xt, S_xt = st["xt"][sl], st["S_xt"][sl]
            P.dma("sync", lambda e: e.dma_start(out=xt, in_=src_rows), st["Dx"][sl], writes=[S_xt])
            j = u % 8
            ssc, S_ss = st["ss"][:, j:j + 1], st["S_ss"][j]
            sdc = st["sd"][:, j:j + 1]
            rsc = st["rs"][:, j:j + 1]
            P.op("scalar", lambda e: e.activation(out=st["junk"], in_=xt, func=AF.Square, accum_out=ssc),
                 reads=[S_xt], writes=[S_ss, st["S_junk"]])
            P.op("scalar", lambda e: e.activation(out=sdc, in_=ssc, func=AF.Sqrt, bias=st["epsc"], scale=1.0 / DM),
                 reads=[S_ss, S_const], writes=[S_ss])
            hr = P.op("vector", lambda e: e.reciprocal(out=rsc, in_=sdc), reads=[S_ss], writes=[S_ss])
            hs = u % 2
            hf, S_hf = st["hf"][hs], st["S_hf"][hs]
            P.op("vector", lambda e: e.scalar_tensor_tensor(out=hf, in0=xt, scalar=rsc, in1=gb,
                                                            op0=ALU.mult, op1=ALU.mult),
                 reads=[S_xt, S_ss, S_gb], writes=[S_hf], selfdeps=[hr])
            if hT_dst is not None:
                norm_p2(hf, S_hf, hT_dst, S_hT)
            return xt, S_xt, hf, S_hf

        def norm_p2(hf, S_hf, hT_dst, S_hT):
            for half in range(2):
                b = nbank(0, 4)
                for cc in range(4):
                    c = half * 4 + cc
                    P.op("tensor", lambda e, b=b, cc=cc, c=c: e.transpose(psb[b][:, cc * 128:(cc + 1) * 128],
                                                                       hf[:, c * 128:(c + 1) * 128], ident),
                         reads=[S_hf, S_const], writes=[PS[b]])
                dst = hT_dst(half * 4)
                src = psb[b][:, 0:512].rearrange("p (c t) -> p c t", c=4)
                if half == 0:
                    P.op("scalar", lambda e, dst=dst, src=src: e.copy(out=dst, in_=src), reads=[PS[b]], writes=[S_hT])
                else:
                    P.op("vector", lambda e, dst=dst, src=src: e.tensor_copy(out=dst, in_=src), reads=[PS[b]], writes=[S_hT])

        def norm_state(nxt):
            st = {"nxt": nxt}
            st["xt"] = [AFa.take(DM) for _ in range(nxt)]
            st["S_xt"] = [Slot() for _ in range(nxt)]
            st["hf"] = [AFa.take(DM) for _ in range(2)]
            st["S_hf"] = [Slot(), Slot()]
            st["ss"] = AFa.take(8)
            st["sd"] = AFa.take(8)
            st["rs"] = AFa.take(8)
            st["epsc"] = AFa.take(1)
            st["S_ss"] = [Slot() for _ in range(8)]
            st["junk"] = AB.take(DM)
            st["S_junk"] = Slot()
            st["Dx"] = [P.sem("dx") for _ in range(nxt)]
            P.op("vector", lambda e: e.memset(st["epsc"], EPS), writes=[S_const])
            return st

        Dw = P.sem("dw")
        Dht = P.sem("dht")
        Dqk = [P.sem("dqk"), P.sem("dqk")]
        Dvs = [P.sem("dvs"), P.sem("dvs")]
        wq = AB.take(8 * 4608).rearrange("p (k c) -> p k c", k=8)
        S_wq = Slot()
        S_wqk = Slot()
        Dw_qk = P.sem("dw")
        for k in range(8):
            cast_load(wq[:, k, 0:3072], w_in_d[k * 128:(k + 1) * 128, 0:3072], S_wqk, Dw_qk, 1536)
        for k in range(8):
            cast_load(wq[:, k, 3072:4608], w_in_d[k * 128:(k + 1) * 128, 3072:4608], S_wq, Dw, 1536)
        gmix = AFa.take(DM)
        S_g = Slot()
        P.dma("sync", lambda e: e.dma_start(out=gmix, in_=gmix_d), P.sem("dc"), writes=[S_g])
        zt = AB.take(4096)
        S_z = Slot()
        P.op("vector", lambda e: e.memset(zt, 0.0), writes=[S_z])
        stA = norm_state(3)
        hT = AB.take(8 * 2048).rearrange("p (c t) -> p c t", c=8)
        S_hT = Slot()
        qkst = [AB.take(6 * 512).rearrange("p (c t) -> p c t", c=6) for _ in range(2)]
        S_qk = [Slot(), Slot()]
        vst = [AB.take(8 * 1024).rearrange("p (a c) -> p a c", a=8) for _ in range(2)]
        S_vs = [Slot(), Slot()]
        for i_ in range(2):
            P.op("vector", lambda e, i_=i_: e.memset(vst[i_], 1.0), writes=[S_vs[i_]])
        hTv = hT_s.rearrange("(c p) t -> p c t", p=128)
        Qv = Q_s.rearrange("(c p) t -> p c t", p=128)
        qkc = [0]
        vsc = [0]
        evc = [0]

        def evac(dst, src, reads, writes, scale=None):
            evc[0] += 1
            if evc[0] % 2 == 0:
                if scale is None:
                    P.op("scalar", lambda e: e.copy(out=dst, in_=src), reads=reads, writes=writes)
                else:
                    P.op("scalar", lambda e: e.mul(out=dst, in_=src, mul=scale), reads=reads, writes=writes)
            else:
                if scale is None:
                    P.op("vector", lambda e: e.tensor_copy(out=dst, in_=src), reads=reads, writes=writes)
                else:
                    P.op("vector", lambda e: e.tensor_scalar_mul(out=dst, in0=src, scalar1=scale), reads=reads, writes=writes)

        S_hTb = [Slot() for _ in range(4)]

        def a_p1(s, u):
            t0 = s * 2048 + u * 128
            return norm_subtile(stA, s * 16 + u, x_d[t0:t0 + 128, :], gmix, S_g, None, None)

        def a_p2(u, nx):
            norm_p2(nx[2], nx[3], lambda c0, u=u: hT[:, c0:c0 + 4, u * 128:(u + 1) * 128], S_hTb[u // 4])

        def a_norm_group(s, bq):
            us = [4 * bq + i for i in range(4)]
            n0 = a_p1(s, us[0])
            n1 = a_p1(s, us[1])
            a_p2(us[0], n0)
            n2 = a_p1(s, us[2])
            a_p2(us[1], n1)
            n3 = a_p1(s, us[3])
            a_p2(us[2], n2)
            a_p2(us[3], n3)

        def a_vtiles(s, g, r, alist):
            d = DIL[g]
            nq = 16 // d
            n = len(alist)
            sl = vsc[0] % 2
            vsc[0] += 1
            for ai, a in enumerate(alist):
                b = nbank()
                blks = sorted(set(((a * 128 + j) * d + r) // 512 for j in (0, 127)))
                rdb = [S_hTb[x] for x in range(blks[0], blks[-1] + 1)]
                for k in range(8):
                    P.op("tensor", lambda e, b=b, k=k, a=a, r=r, d=d, g=g: e.matmul(
                        psb[b][:, :], lhsT=hT[:, k, ssl(a * 128 * d + r, 128, d)],
                        rhs=wq[:, k, 3072 + g * 512:3072 + (g + 1) * 512], start=(k == 0), stop=(k == 7)),
                        reads=[S_wq] + rdb, writes=[PS[b]])
                pv4 = psb[b][:, :].rearrange("p (q h c) -> p q h c", q=4, h=2)
                vd4 = vst[sl][:, ai, :].rearrange("p (q c) -> p q c", q=4)
                evac(vd4[:, :, 0:64], pv4[:, :, 0, :], [PS[b]], [S_vs[sl]])
                evac(vd4[:, :, 192:256], pv4[:, :, 1, :], [PS[b]], [S_vs[sl]])
            ag = 1 + s * nq + alist[0]
            for q4 in range(4):
                P.dma("gpsimd", lambda e, g=g, r=r, ag=ag, n=n, sl=sl, q4=q4: e.dma_start(
                    out=V_s[g][q4, r, :, ag:ag + n, :],
                    in_=vst[sl][:, 0:n, q4 * 256:(q4 + 1) * 256]), Dvs[sl], reads=[S_vs[sl]])

        def a_norm_steps(s, bq):
            us = [4 * bq + i for i in range(4)]
            st_ = {}

            def mk(i):
                def step():
                    if i == 0:
                        st_[0] = a_p1(s, us[0])
                        st_[1] = a_p1(s, us[1])
                    elif i == 1:
                        a_p2(us[0], st_[0])
                        st_[2] = a_p1(s, us[2])
                    elif i == 2:
                        a_p2(us[1], st_[1])
                        st_[3] = a_p1(s, us[3])
                    elif i == 3:
                        a_p2(us[2], st_[2])
                    else:
                        a_p2(us[3], st_[3])
                return step
            return [mk(i) for i in range(5)]

        def a_qk_block(s, blk, steps):
            T0 = s * 2048
            for grp in range(4):
                sl = qkc[0] % 2
                qkc[0] += 1
                for ci in range(6):
                    ct = grp * 6 + ci
                    b = nbank()
                    for k in range(8):
                        P.op("tensor", lambda e, b=b, k=k, ct=ct, blk=blk: e.matmul(
                            psb[b][:, :], lhsT=wq[:, k, ct * 128:(ct + 1) * 128],
                            rhs=hT[:, k, blk * 512:(blk + 1) * 512], start=(k == 0), stop=(k == 7)),
                            reads=[S_wqk, S_hTb[blk]], writes=[PS[b]])
                    evac(qkst[sl][:, ci, :], psb[b][:, :], [PS[b]], [S_qk[sl]],
                         scale=(0.125 if ct < 12 else None))
                tk = T0 + blk * 512
                if grp < 2:
                    P.dma("gpsimd", lambda e, sl=sl, grp=grp, tk=tk: e.dma_start(
                        out=Qv[:, grp * 6:(grp + 1) * 6, tk:tk + 512], in_=qkst[sl]), Dqk[sl], reads=[S_qk[sl]])
                else:
                    P.dma("gpsimd", lambda e, sl=sl, grp=grp, tk=tk: e.dma_start(
                        out=Kv[:, (grp - 2) * 6:(grp - 1) * 6, PADK + tk:PADK + tk + 512], in_=qkst[sl]),
                        Dqk[sl], reads=[S_qk[sl]])
                if steps:
                    steps.pop(0)()
            a_vtiles(s, 0, 0, [4 * blk + i for i in range(4)])
            while steps:
                steps.pop(0)()

        for bq in range(4):
            for st_ in a_norm_steps(0, bq):
                st_()
        pend_steps = []
        for s in range(NS):
            T0 = s * 2048
            a_qk_block(s, 0, pend_steps)
            P.dma("gpsimd", lambda e, T0=T0: e.dma_start(out=hTv[:, :, T0:T0 + 2048], in_=hT), Dht, reads=S_hTb)
            for r in range(16):
                a_vtiles(s, 2, r, [0])
            for r in range(4):
                a_vtiles(s, 1, r, [0, 1, 2, 3])
            nxt = s + 1 < NS
            a_qk_block(s, 1, a_norm_steps(s + 1, 0) if nxt else [])
            a_qk_block(s, 2, a_norm_steps(s + 1, 1) if nxt else [])
            a_qk_block(s, 3, a_norm_steps(s + 1, 2) if nxt else [])
            pend_steps = a_norm_steps(s + 1, 3) if nxt else []

        Dz = P.sem("dz")
        Kv = K_s.rearrange("(c p) t -> p c t", p=128)
        for c in range(12):
            for side in range(2):
                off = 0 if side == 0 else PADK + NT
                P.dma("sync", lambda e, c=c, off=off: e.dma_start(out=Kv[:, c, off:off + PADK], in_=zt[:, 0:PADK]), Dz, reads=[S_z])
        for g in range(3):
            d = DIL[g]
            for pr in range(4):
                for a in (0, NAg[g] - 1):
                    P.dma("sync", lambda e, g=g, pr=pr, a=a, d=d: e.dma_start(
                        out=V_s[g][pr, :, :, a, :].rearrange("r p c -> p r c"),
                        in_=zt[:, 0:d * 256].rearrange("p (r c) -> p r c", r=d)), Dz, reads=[S_z])


        P.barrier()
        P.new_phase()
        AB.reset()
        AFa.o = f32_base
        Dtab = P.sem("dtab")
        NU = 3
        Dv = [P.sem("dv") for _ in range(NU)]
        Dqk2 = [P.sem("dq") for _ in range(NU)]
        Dast = [P.sem("da"), P.sem("da")]
        tab = AB.take(12 * 512).rearrange("p (h c) -> p h c", h=12)
        bnd = AB.take(12 * 256).rearrange("p (h c) -> p h c", h=12)
        negt = AB.take(128)
        idb = AB.take(128)
        S_tab = Slot()
        P.dma("sync", lambda e: e.dma_start(out=tab, in_=tab_d.rearrange("p (h c) -> p h c", h=12)), Dtab, writes=[S_tab])
        P.dma("sync", lambda e: e.dma_start(out=bnd, in_=bnd_d.rearrange("p (h c) -> p h c", h=12)), Dtab, writes=[S_tab])
        P.op("vector", lambda e: e.memset(negt, NEGV), writes=[S_tab])
        P.op("vector", lambda e: e.tensor_copy(out=idb, in_=ident), reads=[S_const], writes=[S_tab])
        assert NU == 3
        Vw = [AB.take(DIL[g_] * (16 // DIL[g_] + 2) * 256) for g_ in range(NU)]
        S_Vw = [Slot() for _ in range(NU)]
        QAB = [AB.take(4096) for _ in range(NU)]
        S_Q = [Slot() for _ in range(NU)]
        QR = [None] + [AB.take(4096) for _ in range(2)]
        S_QR = [Slot() for _ in range(NU)]
        Kw = [AB.take(2048 + 256 * DIL[g_]) for g_ in range(NU)]
        S_K = [Slot() for _ in range(NU)]
        NPT = 4
        Pt = [AB.take(512) for _ in range(NPT)]
        S_Pt = [Slot() for _ in range(NPT)]
        ast = [AB.take(2048) for _ in range(2)]
        S_ast = [Slot(), Slot()]
        acc = [AFa.take(2 * 2048).rearrange("p (h t) -> p h t", h=2) for _ in range(2)]
        S_acc = [[Slot(), Slot(), Slot()], [Slot(), Slot(), Slot()]]
        rec = AFa.take(2048)
        S_rec = Slot()
        qms = []
        for i in range(NU):
            qms.append(P.op("vector", lambda e, i=i: e.memset(QAB[i], 0.0), writes=[S_Q[i]]))
        unit = [0]
        tilec = [0]
        occ = [0]
        pend = []
        LA = 2
        SB0, OB0 = 2, 5

        def pv_stage(job):
            (slv, r, a, g, d, ptsl, first, asl, after) = job
            b = OB0 + ((occ[0] // 2) % 3)
            j2 = occ[0] % 2
            occ[0] += 1
            Vt = Vw[slv][:, 0:d * (16 // d + 2) * 256].rearrange("p (r a c) -> p r a c", r=d, c=256)
            pt = Pt[ptsl]
            rd = [S_Vw[slv], S_Pt[ptsl]]
            for hd in range(2):
                O = psb[b][:, (j2 * 2 + hd) * 128:(j2 * 2 + hd + 1) * 128]
                hc = slice(hd * 128, (hd + 1) * 128)
                P.op("tensor", lambda e, O=O, hc=hc, hd=hd: e.matmul(O[:, 0:128], lhsT=Vt[:, r, a + 1, hc], rhs=pt[:, hd * 128:(hd + 1) * 128], start=True, stop=False), reads=rd, writes=[PS[b]])
                P.op("tensor", lambda e, O=O, hc=hc, hd=hd: e.matmul(O[:, 0:64], lhsT=Vt[:, r, a, hc], rhs=pt[:, 256 + hd * 64:320 + hd * 64], start=False, stop=False), reads=rd, writes=[PS[b]])
                P.op("tensor", lambda e, O=O, hc=hc, hd=hd: e.matmul(O[:, 64:128], lhsT=Vt[:, r, a + 2, hc], rhs=pt[:, 384 + hd * 64:448 + hd * 64], start=False, stop=True), reads=rd, writes=[PS[b]])
            if j2 != 1:
                assert after is None
                return
            src = psb[b][:, 0:512].rearrange("p (j h q) -> p j h q", j=2, h=2)
            A_ = acc[asl]
            if g == 0:
                dst = A_[:, :, (a - 1) * 128:(a + 1) * 128].rearrange("p h (j q) -> p j h q", j=2)
            elif g == 1:
                dst = A_[:, :, ssl((a - 1) * 512 + r, 256, 4)].rearrange("p h (j q) -> p j h q", j=2)
            else:
                dst = A_.rearrange("p h (q r) -> p r h q", r=16)[:, r - 1:r + 1, :, :]
            if first:
                P.op("vector", lambda e: e.tensor_copy(out=dst, in_=src), reads=[PS[b]], writes=[S_acc[asl][0], S_acc[asl][2]])
            else:
                P.op("vector", lambda e: e.tensor_tensor(out=dst, in0=src, in1=dst, op=ALU.add),
                     reads=[PS[b], S_acc[asl][g - 1]], writes=[S_acc[asl][g]])
            if after is not None:
                after()

        norm_steps = []

        def finish_pair(s, pr, asl):
            A_ = acc[asl]
            SA = S_acc[asl][2]
            for c4 in range(4):
                cs = slice(c4 * 512, (c4 + 1) * 512)
                norm_steps.append(lambda cs=cs: P.op("scalar", lambda e: e.activation(
                    out=rec[0:64, cs], in_=A_[64:128, 0, cs], func=AF.Ln), reads=[SA], writes=[S_rec]))
                norm_steps.append(lambda cs=cs: P.op("scalar", lambda e: e.activation(
                    out=rec[64:128, cs], in_=A_[0:64, 1, cs], func=AF.Ln), reads=[SA], writes=[S_rec]))
                norm_steps.append(lambda cs=cs: P.op("scalar", lambda e: e.activation(
                    out=rec[:, cs], in_=rec[:, cs], func=AF.Exp, scale=-1.0), reads=[S_rec], writes=[S_rec]))

            def fin():
                P.op("gpsimd", lambda e: e.tensor_tensor(
                    out=ast[asl][0:64, :], in0=A_[0:64, 0, :], in1=rec[0:64, :], op=ALU.mult),
                    reads=[SA, S_rec], writes=[S_ast[asl]])
                P.op("gpsimd", lambda e: e.tensor_tensor(
                    out=ast[asl][64:128, :], in0=A_[64:128, 1, :], in1=rec[64:128, :], op=ALU.mult),
                    reads=[SA, S_rec], writes=[S_ast[asl]])
                P.dma("gpsimd", lambda e: e.dma_start(
                    out=at_s[pr * 128:(pr + 1) * 128, s * 2048:(s + 1) * 2048], in_=ast[asl]), Dast[asl], reads=[S_ast[asl]])
            norm_steps.append(fin)

        units = [(s_, pr_, g_) for s_ in range(NS) for pr_ in range(4) for g_ in range(3)]

        def unit_loads(u):
            if u >= len(units):
                return
            s, pr, g = units[u]
            d = DIL[g]
            nq = 16 // d
            sl = g
            nel = d * (nq + 2) * 256
            Vt4 = Vw[sl][:, 0:nel].rearrange("p (r a c) -> p r a c", r=d, c=256)
            P.dma("sync", lambda e: e.dma_start(
                out=Vt4, in_=V_s[g][pr].rearrange("r p a c -> p r a c")[:, :, s * nq:s * nq + nq + 2, :]),
                Dv[sl], writes=[S_Vw[sl]])
            row0 = g * 512 + pr * 128
            P.dma("sync", lambda e: e.dma_start(
                out=QAB[sl][0:64, 0:2048], in_=Q_s[row0:row0 + 64, s * 2048:(s + 1) * 2048]), Dqk2[sl], writes=[S_Q[sl]], deps=qms)
            P.dma("sync", lambda e: e.dma_start(
                out=QAB[sl][64:128, 2048:4096], in_=Q_s[row0 + 64:row0 + 128, s * 2048:(s + 1) * 2048]), Dqk2[sl], writes=[S_Q[sl]], deps=qms)
            kw = 2048 + 256 * d
            k0 = PADK + s * 2048 - 128 * d
            P.dma("sync", lambda e: e.dma_start(
                out=Kw[sl][:, 0:kw], in_=K_s[row0:row0 + 128, k0:k0 + kw]), Dqk2[sl], writes=[S_K[sl]])

        def unit_copy(u):
            if u >= len(units):
                return
            s, pr, g = units[u]
            if g == 0:
                return
            d = DIL[g]
            src = QAB[g].rearrange("p (h m r) -> p h r m", h=2, r=d)
            dst = QR[g].rearrange("p (h r m) -> p h r m", h=2, r=d)
            P.op("scalar", lambda e: e.copy(out=dst[:, 0], in_=src[:, 0]), reads=[S_Q[g]], writes=[S_QR[g]])
            P.op("vector", lambda e: e.tensor_copy(out=dst[:, 1], in_=src[:, 1]), reads=[S_Q[g]], writes=[S_QR[g]])

        unit_loads(0)
        unit_loads(1)
        unit_copy(0)
        for u_, (s, pr, g) in enumerate(units):
            if True:
                asl = (s * 4 + pr) % 2
                if True:
                    d = DIL[g]
                    nq = 16 // d
                    sl = g
                    unit_copy(u_ + 1)
                    jn = 0
                    gh = g * 4 + pr
                    if g == 0:
                        q2 = QAB[sl].rearrange("p (h t) -> p h t", h=2)
                        S_q = S_Q[sl]
                    else:
                        q2r = QR[sl].rearrange("p (h r m) -> p h r m", h=2, r=d)
                        S_q = S_QR[sl]
                    for r in range(d):
                        for a in range(nq):
                            qs = a * 128 * d + r
                            b = SB0 + (tilec[0] % 3)
                            ptsl = tilec[0] % NPT
                            tilec[0] += 1
                            S3 = psb[b]
                            kc = lambda ka, sl=sl, d=d, r=r: Kw[sl][:, ssl(128 * d + ka * 128 * d + r, 128, d)]
                            rd = [S_q, S_K[sl]]
                            if g == 0:
                                qm, ql, qr_ = q2[:, :, ssl(qs, 128, d)], q2[:, :, ssl(qs, 64, d)], q2[:, :, ssl(qs + 64 * d, 64, d)]
                            else:
                                qm = q2r[:, :, r, a * 128:a * 128 + 128]
                                ql = q2r[:, :, r, a * 128:a * 128 + 64]
                                qr_ = q2r[:, :, r, a * 128 + 64:a * 128 + 128]
                            P.op("tensor", lambda e, S3=S3, kc=kc, a=a, qm=qm: e.matmul(
                                S3[:, 0:256], lhsT=kc(a), rhs=qm, start=True, stop=False),
                                reads=rd, writes=[PS[b]])
                            P.op("tensor", lambda e, S3=S3, kc=kc, a=a, ql=ql: e.matmul(
                                S3[:, 256:384], lhsT=kc(a - 1), rhs=ql, start=False, stop=False),
                                reads=rd, writes=[PS[b]])
                            P.op("tensor", lambda e, S3=S3, kc=kc, a=a, qr_=qr_: e.matmul(
                                S3[:, 384:512], lhsT=kc(a + 1), rhs=qr_, start=False, stop=False),
                                reads=rd, writes=[PS[b]])
                            lmode = rmode = 0
                            if a == 0 and s % 2 == 0:
                                lmode = 2 if s == 2 else 1
                            if a == nq - 1 and s % 2 == 1:
                                rmode = 2 if s == 1 else 1
                            if s == NS - 1 and a == nq - 1 and rmode == 2:
                                rmode = 1
                            if lmode == 0 and rmode == 0:
                                P.op("tensor", lambda e, b=b, gh=gh: e.matmul(
                                    psb[b][:, 0:512], lhsT=idb, rhs=tab[:, gh, :], start=False, stop=True),
                                    reads=[S_tab], writes=[PS[b]])
                            else:
                                lsrc = [tab[:, gh, 256:384], negt, bnd[:, gh, 0:128]][lmode]
                                rsrc = [tab[:, gh, 384:512], negt, bnd[:, gh, 128:256]][rmode]
                                P.op("tensor", lambda e, S3=S3, gh=gh: e.matmul(
                                    S3[:, 0:256], lhsT=idb, rhs=tab[:, gh, 0:256], start=False, stop=False),
                                    reads=[S_tab], writes=[PS[b]])
                                P.op("tensor", lambda e, S3=S3, lsrc=lsrc: e.matmul(
                                    S3[:, 256:384], lhsT=idb, rhs=lsrc, start=False, stop=False),
                                    reads=[S_tab], writes=[PS[b]])
                                P.op("tensor", lambda e, S3=S3, rsrc=rsrc: e.matmul(
                                    S3[:, 384:512], lhsT=idb, rhs=rsrc, start=False, stop=True),
                                    reads=[S_tab], writes=[PS[b]])
                            P.op("scalar", lambda e, b=b, ptsl=ptsl: e.activation(out=Pt[ptsl], in_=psb[b][:, 0:512], func=AF.Exp),
                                 reads=[PS[b]], writes=[S_Pt[ptsl]])
                            last = (g == 2 and r == d - 1 and a == nq - 1)
                            pend.append((sl, r, a, g, d, ptsl, g == 0, asl,
                                         (lambda s=s, pr=pr, asl=asl: finish_pair(s, pr, asl)) if last else None))
                            if len(pend) > LA:
                                pv_stage(pend.pop(0))
                            if norm_steps:
                                norm_steps.pop(0)()
                            if jn == LA:
                                unit_loads(u_ + 2)
                            jn += 1
        while pend:
            pv_stage(pend.pop(0))
        while norm_steps:
            norm_steps.pop(0)()

        P.barrier()
        P.new_phase()
        AB.reset()
        AFa.o = f32_base
        Dw2 = P.sem("dw")
        Dlh = [P.sem("dlh"), P.sem("dlh")]
        Dla = [P.sem("dla"), P.sem("dla")]
        Dx2 = [P.sem("dx"), P.sem("dx"), P.sem("dx")]
        Dst3 = [P.sem("dst"), P.sem("dst")]
        S_w2 = Slot()
        wB = AB.take(8 * 4096).rearrange("p (k c) -> p k c", k=8)
        wa = AB.take(4 * 1024).rearrange("p (k c) -> p k c", k=4)
        wb = AB.take(8 * 1024).rearrange("p (k c) -> p k c", k=8)
        wo = AB.take(8 * 1024).rearrange("p (k c) -> p k c", k=8)
        wsT_f = AB.take(1024)
        wsT = wsT_f.rearrange("p (g t) -> p g t", g=8)
        bsr = AB.take(1024)
        onesr = AB.take(128)
        S_wv, S_wu, S_ws, S_wg, S_wo = Slot(), Slot(), Slot(), Slot(), Slot()
        Dwv, Dwu, Dws, Dwg, Dwo = P.sem("dw"), P.sem("dw"), P.sem("dw"), P.sem("dw"), P.sem("dw")
        for k in range(8):
            cast_load(wB[:, k, 1024:2048], w_in_d[k * 128:(k + 1) * 128, 5632:6656], S_wv, Dwv)
        for k in range(8):
            cast_load(wB[:, k, 0:1024], w_in_d[k * 128:(k + 1) * 128, 4608:5632], S_wu, Dwu)
        cast_load(wsT_f, wsT_d, S_ws, Dws)
        P.dma("gpsimd", lambda e: e.dma_start(out=bsr[0:1, :], in_=bs_d), Dws, writes=[S_ws])
        P.op("vector", lambda e: e.memset(onesr, 1.0), writes=[S_ws])
        for k in range(4):
            cast_load(wa[:, k, :], wa_d[k * 128:(k + 1) * 128, :], S_wg, Dwg)
        for k in range(8):
            cast_load(wb[:, k, :], wb_d[k * 128:(k + 1) * 128, :], S_wg, Dwg)
            cast_load(wB[:, k, 2048:4096], w_in_d[k * 128:(k + 1) * 128, 6656:8704], S_wg, Dwg)
        for k in range(8):
            cast_load(wo[:, k, :], wo_d[k * 128:(k + 1) * 128, :], S_wo, Dwo)
        gsgu = AFa.take(DM)
        P.dma("sync", lambda e: e.dma_start(out=gsgu, in_=gsgu_d), P.sem("dc"), writes=[S_g])
        hTt = [AB.take(8 * 512).rearrange("p (c t) -> p c t", c=8) for _ in range(2)]
        S_hTt = [Slot(), Slot()]
        att = [AB.take(4 * 512).rearrange("p (c t) -> p c t", c=4) for _ in range(2)]
        S_att = [Slot(), Slot()]
        uT = AB.take(8 * 512).rearrange("p (c t) -> p c t", c=8)
        S_uT = Slot()
        vn = AB.take(4 * 1024).rearrange("p (c f) -> p c f", c=4)
        S_vn = Slot()
        mg = AB.take(8 * 512).rearrange("p (c t) -> p c t", c=8)
        S_mg = Slot()
        junk2 = AB.take(1024)
        S_j2 = Slot()
        G = [AFa.take(1024) for _ in range(2)]
        S_G = [Slot(), Slot()]
        sg = [AFa.take(512) for _ in range(2)]
        S_sg = [Slot(), Slot()]
        tt = [AFa.take(512) for _ in range(2)]
        S_tt = [Slot(), Slot()]
        xc = [AFa.take(1024) for _ in range(3)]
        S_xc = [Slot(), Slot(), Slot()]
        x1c = [AFa.take(1024) for _ in range(2)]
        S_x1 = [Slot(), Slot()]
        st2 = AFa.take(24)
        eps2 = AFa.take(1)
        P.op("vector", lambda e: e.memset(eps2, EPS), writes=[S_const])
        S_st2 = [Slot() for _ in range(8)]
        atv = at_s.rearrange("(c p) t -> p c t", p=128)
        cnt = [0]
        NTL = NT // 512

        def b2_loads(i):
            tk = i * 512
            hs = i % 2
            P.dma("sync", lambda e, hs=hs, tk=tk: e.dma_start(out=hTt[hs], in_=hTv[:, :, tk:tk + 512]), Dlh[hs], writes=[S_hTt[hs]])
            P.dma("sync", lambda e, hs=hs, tk=tk: e.dma_start(out=att[hs], in_=atv[:, :, tk:tk + 512]), Dla[hs], writes=[S_att[hs]])

        def b2_u(i, js):
            tk = i * 512
            hs = i % 2
            hTi, ati = hTt[hs], att[hs]
            for j in js:
                b = nbank()
                for k in range(8):
                    P.op("tensor", lambda e, b=b, k=k, j=j, hTi=hTi: e.matmul(psb[b][:, :], lhsT=wB[:, k, j * 128:(j + 1) * 128],
                                                                      rhs=hTi[:, k, :], start=(k == 0), stop=(k == 7)),
                         reads=[S_wu, S_hTt[hs]], writes=[PS[b]])
                P.op("scalar", lambda e, b=b, j=j: e.activation(out=uT[:, j, :], in_=psb[b][:, :], func=AF.Gelu),
                     reads=[PS[b]], writes=[S_uT])

        def b2_v(i):
            tk = i * 512
            hs = i % 2
            hTi, ati = hTt[hs], att[hs]
            for c in range(4):
                gs = cnt[0] % 2
                jj = cnt[0] % 8
                cnt[0] += 1
                for half in range(2):
                    b = nbank()
                    for k in range(8):
                        P.op("tensor", lambda e, b=b, k=k, c=c, half=half, hTi=hTi: e.matmul(
                            psb[b][:, :], lhsT=hTi[:, k, c * 128:(c + 1) * 128],
                            rhs=wB[:, k, 1024 + half * 512:1024 + (half + 1) * 512], start=(k == 0), stop=(k == 7)),
                            reads=[S_wv, S_hTt[hs]], writes=[PS[b]])
                    P.op("scalar", lambda e, b=b, gs=gs, half=half: e.activation(
                        out=G[gs][:, half * 512:(half + 1) * 512], in_=psb[b][:, :], func=AF.Gelu),
                        reads=[PS[b]], writes=[S_G[gs]])
                ssc, sdc, rsc = st2[:, jj:jj + 1], st2[:, 8 + jj:9 + jj], st2[:, 16 + jj:17 + jj]
                P.op("scalar", lambda e, gs=gs, ssc=ssc: e.activation(out=junk2, in_=G[gs], func=AF.Square, accum_out=ssc),
                     reads=[S_G[gs]], writes=[S_st2[jj], S_j2])
                P.op("scalar", lambda e, ssc=ssc, sdc=sdc: e.activation(out=sdc, in_=ssc, func=AF.Sqrt, bias=eps2, scale=1.0 / DM),
                     reads=[S_const, S_st2[jj]], writes=[S_st2[jj]])
                hr = P.op("vector", lambda e, sdc=sdc, rsc=rsc: e.reciprocal(out=rsc, in_=sdc), reads=[S_st2[jj]], writes=[S_st2[jj]])
                P.op("vector", lambda e, gs=gs, rsc=rsc, c=c: e.scalar_tensor_tensor(
                    out=vn[:, c, :], in0=G[gs], scalar=rsc, in1=gsgu, op0=ALU.mult, op1=ALU.mult),
                    reads=[S_G[gs], S_st2[jj], S_g], writes=[S_vn], selfdeps=[hr])

        def b2_s(i, gis):
            tk = i * 512
            hs = i % 2
            hTi, ati = hTt[hs], att[hs]
            for gi in gis:
                b = nbank()
                for c in range(4):
                    P.op("tensor", lambda e, b=b, c=c, gi=gi: e.matmul(
                        psb[b][:, c * 128:(c + 1) * 128], lhsT=vn[:, c, gi * 128:(gi + 1) * 128], rhs=wsT[:, gi, :],
                        start=True, stop=False), reads=[S_vn, S_ws], writes=[PS[b]])
                    P.op("tensor", lambda e, b=b, c=c, gi=gi: e.matmul(
                        psb[b][:, c * 128:(c + 1) * 128], lhsT=onesr[0:1, :], rhs=bsr[0:1, gi * 128:(gi + 1) * 128],
                        start=False, stop=True), reads=[S_ws], writes=[PS[b]])
                P.op("vector", lambda e, b=b, gi=gi: e.tensor_tensor(out=uT[:, gi, :], in0=psb[b][:, :], in1=uT[:, gi, :], op=ALU.mult),
                     reads=[PS[b]], writes=[S_uT])

        def b2_b(i):
            tk = i * 512
            hs = i % 2
            hTi, ati = hTt[hs], att[hs]
            for j in range(8):
                bA, bB, bGa, bGb = nbank(), nbank(), nbank(), nbank()
                for k in range(4):
                    P.op("tensor", lambda e, k=k, j=j, bA=bA, ati=ati: e.matmul(psb[bA][:, :], lhsT=wa[:, k, j * 128:(j + 1) * 128],
                                                                        rhs=ati[:, k, :], start=(k == 0), stop=(k == 3)),
                         reads=[S_wg, S_att[hs]], writes=[PS[bA]])
                for k in range(8):
                    P.op("tensor", lambda e, k=k, j=j, bB=bB: e.matmul(psb[bB][:, :], lhsT=wb[:, k, j * 128:(j + 1) * 128],
                                                                        rhs=uT[:, k, :], start=(k == 0), stop=(k == 7)),
                         reads=[S_wg, S_uT], writes=[PS[bB]])
                for (bb, off) in ((bGa, 2048), (bGb, 3072)):
                    for k in range(8):
                        P.op("tensor", lambda e, k=k, j=j, bb=bb, off=off, hTi=hTi: e.matmul(
                            psb[bb][:, :], lhsT=wB[:, k, off + j * 128:off + (j + 1) * 128], rhs=hTi[:, k, :],
                            start=(k == 0), stop=(k == 7)), reads=[S_wg, S_hTt[hs]], writes=[PS[bb]])
                P.op("scalar", lambda e, bGa=bGa: e.activation(out=sg[0], in_=psb[bGa][:, :], func=AF.Sigmoid),
                     reads=[PS[bGa]], writes=[S_sg[0]])
                P.op("scalar", lambda e, bGb=bGb: e.activation(out=sg[1], in_=psb[bGb][:, :], func=AF.Sigmoid),
                     reads=[PS[bGb]], writes=[S_sg[1]])
                P.op("vector", lambda e, bA=bA: e.tensor_tensor(out=tt[0], in0=psb[bA][:, :], in1=sg[0], op=ALU.mult),
                     reads=[PS[bA], S_sg[0]], writes=[S_tt[0]])
                P.op("vector", lambda e, bB=bB: e.tensor_tensor(out=tt[1], in0=psb[bB][:, :], in1=sg[1], op=ALU.mult),
                     reads=[PS[bB], S_sg[1]], writes=[S_tt[1]])
                P.op("gpsimd", lambda e, j=j: e.tensor_tensor(out=mg[:, j, :], in0=tt[0], in1=tt[1], op=ALU.add),
                     reads=[S_tt[0], S_tt[1]], writes=[S_mg])

        def b2_o(i):
            tk = i * 512
            hs = i % 2
            hTi, ati = hTt[hs], att[hs]
            for c in range(4):
                xs = (i * 4 + c) % 2
                x3 = (i * 4 + c) % 3
                t0 = tk + c * 128
                P.dma("sync", lambda e, x3=x3, t0=t0: e.dma_start(out=xc[x3], in_=x_d[t0:t0 + 128, :]), Dx2[x3], writes=[S_xc[x3]])
                for half in range(2):
                    b = nbank()
                    for k in range(8):
                        P.op("tensor", lambda e, b=b, k=k, c=c, half=half: e.matmul(
                            psb[b][:, :], lhsT=mg[:, k, c * 128:(c + 1) * 128], rhs=wo[:, k, half * 512:(half + 1) * 512],
                            start=(k == 0), stop=(k == 7)), reads=[S_wo, S_mg], writes=[PS[b]])
                    P.op("vector", lambda e, b=b, xs=xs, x3=x3, half=half: e.tensor_tensor(
                        out=x1c[xs][:, half * 512:(half + 1) * 512], in0=psb[b][:, :], in1=xc[x3][:, half * 512:(half + 1) * 512],
                        op=ALU.add), reads=[PS[b], S_xc[x3]], writes=[S_x1[xs]])
                P.dma("gpsimd", lambda e, xs=xs, t0=t0: e.dma_start(out=x1_s[t0:t0 + 128, :], in_=x1c[xs]), Dst3[xs], reads=[S_x1[xs]])


        b2_loads(0)
        b2_v(0)
        for i in range(NTL):
            b2_u(i, [0, 1])
            for j_ in range(2, 8):
                b2_s(i, [j_ - 2])
                b2_u(i, [j_])
            b2_s(i, [6, 7])
            b2_b(i)
            if i + 1 < NTL:
                b2_loads(i + 1)
                b2_v(i + 1)
            b2_o(i)

        P.barrier()
        P.new_phase()
        AB.reset()
        AFa.o = f32_base
        Dw3 = P.sem("dw")
        Dst4 = [P.sem("dst"), P.sem("dst")]
        S_w3 = Slot()
        wup = AB.take(8 * 4096).rearrange("p (k c) -> p k c", k=8)
        wdn = AB.take(32 * 1024).rearrange("p (k c) -> p k c", k=32)
        S_wdn = Slot()
        Dwdn = P.sem("dw")
        for k in range(8):
            cast_load(wup[:, k, :], wup_d[k * 128:(k + 1) * 128, :], S_w3, Dw3)
        for k in range(32):
            cast_load(wdn[:, k, :], wdn_d[k * 128:(k + 1) * 128, :], S_wdn, Dwdn)
        gmlp = AFa.take(DM)
        gfin = AFa.take(DM)
        P.dma("sync", lambda e: e.dma_start(out=gmlp, in_=gmlp_d), P.sem("dc"), writes=[S_g])
        P.dma("sync", lambda e: e.dma_start(out=gfin, in_=gfin_d), P.sem("dc"), writes=[S_g])
        stC = norm_state(4)
        h2T = [AB.take(8 * 256).rearrange("p (c t) -> p c t", c=8) for _ in range(2)]
        S_h2 = [Slot(), Slot()]
        upT = AB.take(32 * 256).rearrange("p (c t) -> p c t", c=32)
        S_up = Slot()
        sq = [AFa.take(256) for _ in range(2)]
        S_sq = [Slot(), Slot()]
        x2 = [AFa.take(1024) for _ in range(2)]
        S_x2 = [Slot(), Slot()]
        st3 = AFa.take(24)
        S_st3 = [Slot() for _ in range(8)]
        junk3 = AB.take(1024)
        S_j3 = Slot()
        NTC = NT // 256
        cc2 = [0]
        def c_p1(i):
            return [norm_subtile(stC, i * 2 + c, x1_s[i * 256 + c * 128:i * 256 + (c + 1) * 128, :], gmlp, S_g, None, None)
                    for c in range(2)]

        def c_p2(i, nx):
            for c in range(2):
                norm_p2(nx[c][2], nx[c][3],
                        lambda c0, c=c, i=i: h2T[i % 2][:, c0:c0 + 4, c * 128:(c + 1) * 128], S_h2[i % 2])

        nxt_x = c_p1(0)
        c_p2(0, nxt_x)
        for i in range(NTC):
            hs = i % 2
            xts = nxt_x
            for j in range(32):
                if j == 16 and i + 1 < NTC:
                    nxt_x = c_p1(i + 1)
                b = nbank()
                for k in range(8):
                    P.op("tensor", lambda e, b=b, k=k, j=j, hs=hs: e.matmul(
                        psb[b][:, 0:256], lhsT=wup[:, k, j * 128:(j + 1) * 128], rhs=h2T[hs][:, k, :],
                        start=(k == 0), stop=(k == 7)), reads=[S_w3, S_h2[hs]], writes=[PS[b]])
                qs_ = j % 2
                P.op("scalar", lambda e, b=b, qs_=qs_: e.activation(out=sq[qs_], in_=psb[b][:, 0:256], func=AF.Square),
                     reads=[PS[b]], writes=[S_sq[qs_]])
                P.op("vector", lambda e, b=b, qs_=qs_, j=j: e.scalar_tensor_tensor(
                    out=upT[:, j, :], in0=psb[b][:, 0:256], scalar=0.0, in1=sq[qs_], op0=ALU.is_gt, op1=ALU.mult),
                    reads=[PS[b], S_sq[qs_]], writes=[S_up])
            for c in range(2):
                if c == 1 and i + 1 < NTC:
                    c_p2(i + 1, nxt_x)
                t0 = i * 256 + c * 128
                xt, S_xt = xts[c][0], xts[c][1]
                xs = cc2[0] % 2
                jj = cc2[0] % 8
                cc2[0] += 1
                for half in range(2):
                    b = nbank()
                    for k in range(32):
                        P.op("tensor", lambda e, b=b, k=k, c=c, half=half: e.matmul(
                            psb[b][:, :], lhsT=upT[:, k, c * 128:(c + 1) * 128], rhs=wdn[:, k, half * 512:(half + 1) * 512],
                            start=(k == 0), stop=(k == 31)), reads=[S_wdn, S_up], writes=[PS[b]])
                    P.op("vector", lambda e, b=b, xs=xs, half=half, xt=xt: e.tensor_tensor(
                        out=x2[xs][:, half * 512:(half + 1) * 512], in0=psb[b][:, :], in1=xt[:, half * 512:(half + 1) * 512],
                        op=ALU.add), reads=[PS[b], S_xt], writes=[S_x2[xs]])
                ssc, sdc, rsc = st3[:, jj:jj + 1], st3[:, 8 + jj:9 + jj], st3[:, 16 + jj:17 + jj]
                P.op("scalar", lambda e, xs=xs, ssc=ssc: e.activation(out=junk3, in_=x2[xs], func=AF.Square, accum_out=ssc),
                     reads=[S_x2[xs]], writes=[S_st3[jj], S_j3])
                P.op("scalar", lambda e, ssc=ssc, sdc=sdc: e.activation(out=sdc, in_=ssc, func=AF.Sqrt, bias=stC["epsc"], scale=1.0 / DM),
                     reads=[S_const, S_st3[jj]], writes=[S_st3[jj]])
                hr = P.op("vector", lambda e, sdc=sdc, rsc=rsc: e.reciprocal(out=rsc, in_=sdc), reads=[S_st3[jj]], writes=[S_st3[jj]])
                P.op("vector", lambda e, xs=xs, rsc=rsc: e.scalar_tensor_tensor(
                    out=x2[xs], in0=x2[xs], scalar=rsc, in1=gfin, op0=ALU.mult, op1=ALU.mult),
                    reads=[S_st3[jj], S_g, S_x2[xs]], writes=[S_x2[xs]], selfdeps=[hr])
                P.dma("gpsimd", lambda e, xs=xs, t0=t0: e.dma_start(out=y_d[t0:t0 + 128, :], in_=x2[xs]), Dst4[xs], reads=[S_x2[xs]])

        P.barrier()
        P.emit()
    return nc


def _tables():
    slopes = (2.0 ** (-8.0 * np.arange(1, 25) / 24)).astype(np.float32)
    k = np.arange(128)[:, None].astype(np.float32)
    tab = np.full((128, 12, 512), NEGV, np.float32)
    for g in range(3):
        for pr in range(4):
            for hd in range(2):
                c = slopes[g * 8 + pr * 2 + hd] * DIL[g]
                q = np.arange(128)[None, :].astype(np.float32)
                rel = np.abs(k - q)
                tab[:, g * 4 + pr, hd * 128:(hd + 1) * 128] = np.where(rel <= 64, -c * rel, NEGV)
                q = np.arange(64)[None, :].astype(np.float32)
                dist = 128 + q - k
                tab[:, g * 4 + pr, 256 + hd * 64:320 + hd * 64] = np.where(dist <= 64, -c * dist, NEGV)
                dist = 64 + k - q
                tab[:, g * 4 + pr, 384 + hd * 64:448 + hd * 64] = np.where(dist <= 64, -c * dist, NEGV)
    return tab


_CACHE = {}


def _get_nc(nseg):
    if nseg not in _CACHE:
        _CACHE[nseg] = build(nseg)
    return _CACHE[nseg]


def run_cores(xs, cont, w, nseg):
    tab = _tables()
    tabb = tab.reshape(128, 12 * 512).astype(ml_dtypes.bfloat16)
    bnd_c = np.ascontiguousarray(tab[:, :, 256:512]).reshape(128, 12 * 256).astype(ml_dtypes.bfloat16)
    bnd_m = np.full((128, 12 * 256), NEGV, np.float32).astype(ml_dtypes.bfloat16)
    rep = lambda v: np.ascontiguousarray(np.broadcast_to(np.asarray(v, np.float32).reshape(1, DM), (128, DM)))
    common = {
        "w_in": np.ascontiguousarray(w["w_in"][0]),
        "g_mix_b": rep(w["g_mix"][0]), "g_sgu_b": rep(w["g_sgu"][0]),
        "g_mlp_b": rep(w["g_mlp"][0]), "g_fin_b": rep(w["g_final"]),
        "w_sT": np.ascontiguousarray(np.transpose(w["w_s"][0], (2, 0, 1)).reshape(128, 1024)),
        "b_s": np.ascontiguousarray(w["b_s"][0].reshape(1, 1024)),
        "w_a": np.ascontiguousarray(w["w_branch_a"][0]), "w_b": np.ascontiguousarray(w["w_branch_b"][0]),
        "w_o": np.ascontiguousarray(w["w_out"][0]),
        "w_up": np.ascontiguousarray(w["w_up"][0]), "w_dn": np.ascontiguousarray(w["w_down"][0]),
        "ident": np.eye(128, dtype=np.float32), "tab": tabb,
    }
    in_maps = []
    for x, c in zip(xs, cont):
        m = dict(common)
        m["x"] = np.ascontiguousarray(x)
        m["bnd"] = bnd_c if c else bnd_m
        in_maps.append(m)
    nc = _get_nc(nseg)
    res = run_bass_kernel_spmd(nc, in_maps, core_ids=list(range(len(xs))))
    if DEBUG:
        return res.results
    return [r["y"] for r in res.results]


def kernel(x_prompt, x_sample, g_mix, w_in, w_s, b_s, g_sgu, w_branch_a, w_branch_b, w_out,
           g_mlp, w_up, w_down, g_final):
    w = dict(g_mix=np.asarray(g_mix), w_in=np.asarray(w_in), w_s=np.asarray(w_s), b_s=np.asarray(b_s),
             g_sgu=np.asarray(g_sgu), w_branch_a=np.asarray(w_branch_a), w_branch_b=np.asarray(w_branch_b),
             w_out=np.asarray(w_out), g_mlp=np.asarray(g_mlp), w_up=np.asarray(w_up), w_down=np.asarray(w_down),
             g_final=np.asarray(g_final))
    xp = np.asarray(x_prompt, np.float32)
    xs_ = np.asarray(x_sample, np.float32)
    xs, cont = [], []
    for c in range(4):
        xs.append(np.concatenate([xs_[c], xp[c]], axis=0))
        cont.append(True)
    for c in range(4):
        xs.append(np.concatenate([xp[4 + 3 * c], xp[5 + 3 * c], xp[6 + 3 * c]], axis=0))
        cont.append(False)
    ys = run_cores(xs, cont, w, 3)
    y_p = np.empty_like(xp)
    y_s = np.empty_like(xs_)
    for c in range(4):
        y_s[c] = ys[c][0:8192]
        y_p[c] = ys[c][8192:12288]
    for c in range(4):
        for j in range(3):
            y_p[4 + 3 * c + j] = ys[4 + c][j * 4096:(j + 1) * 4096]
    return (y_p, y_s)
```

```python
import contextlib
import numpy as np
import ml_dtypes
import concourse.bass as bass
import concourse.mybir as mybir
from concourse.bass_utils import run_bass_kernel_spmd

F32, BF16 = mybir.dt.float32, mybir.dt.bfloat16
AF = mybir.ActivationFunctionType
ALU = mybir.AluOpType

DM = 1024
DIL = (1, 4, 16)
PADK = 2048
NEGV = -30000.0
EPS = 1e-6


class Sem:
    def __init__(self, h):
        self.h = h
        self.n = 0


class Slot:
    def __init__(self):
        self.w = {}
        self.r = {}


class Prog:
    ENGS = ("sync", "scalar", "vector", "gpsimd", "tensor")

    def __init__(self, nc, stack):
        self.nc, self.stack = nc, stack
        self.q = {e: [] for e in self.ENGS}
        self.waited = {e: {} for e in self.ENGS}
        self.psem = {}
        self.nsem = 0
        self.allsems = []
        self.new_phase()

    def sem(self, name):
        self.nsem += 1
        s = Sem(self.stack.enter_context(self.nc.semaphore(f"{name}{self.nsem}")))
        self.allsems.append(s)
        return s

    def new_phase(self):
        for e in ("scalar", "vector", "tensor", "gpsimd"):
            self.psem[e] = self.sem("p" + e[0])

    def _waits(self, eng, reads, writes, deps):
        own = self.psem.get(eng)
        need = {}
        for sl in reads:
            for k, v in sl.w.items():
                if k is own and eng == "tensor":
                    continue
                need[k] = max(need.get(k, 0), v)
        for sl in writes:
            for k, v in sl.r.items():
                if k is not own:
                    need[k] = max(need.get(k, 0), v)
            for k, v in sl.w.items():
                if k is not own:
                    need[k] = max(need.get(k, 0), v)
        for (k, v) in deps:
            if k is not own:
                need[k] = max(need.get(k, 0), v)
        waits = []
        for k, v in need.items():
            if self.waited[eng].get(k, 0) < v:
                waits.append((k, v))
                self.waited[eng][k] = v
        return waits

    def op(self, eng, fn, reads=(), writes=(), deps=(), selfdeps=()):
        waits = self._waits(eng, reads, writes, deps)
        own = self.psem[eng]
        for (k, v) in selfdeps:
            assert k is own
            if self.waited[eng].get(k, 0) < v:
                waits.append((k, v))
                self.waited[eng][k] = v
        own.n += 1
        self.q[eng].append((fn, waits, own, 1))
        for sl in reads:
            sl.r[own] = own.n
        for sl in writes:
            sl.w[own] = own.n
        return (own, own.n)

    def dma(self, eng, fn, dsem, reads=(), writes=(), deps=(), nowaw=True):
        if nowaw:
            need_w = []
            waits = self._waits(eng, reads, (), deps)
            need = {}
            for sl in writes:
                for k, v in sl.r.items():
                    need[k] = max(need.get(k, 0), v)
            for k, v in need.items():
                if self.waited[eng].get(k, 0) < v:
                    waits.append((k, v))
                    self.waited[eng][k] = v
        else:
            waits = self._waits(eng, reads, writes, deps)
        dsem.n += 16
        self.q[eng].append((fn, waits, dsem, 16))
        for sl in reads:
            sl.r[dsem] = dsem.n
        for sl in writes:
            sl.w[dsem] = dsem.n
        return (dsem, dsem.n)

    def barrier(self):
        for e in self.ENGS:
            waits = []
            own = self.psem.get(e)
            for s in self.allsems:
                if s is own or s.n == 0:
                    continue
                if self.waited[e].get(s, 0) < s.n:
                    waits.append((s, s.n))
                    self.waited[e][s] = s.n
            if waits:
                self.q[e].append((None, waits, None, 0))

    def emit(self):
        with self.nc.Block() as block:
            for e in self.ENGS:
                items = self.q[e]

                def body(eng, items=items):
                    for fn, waits, sem, inc in items:
                        for (k, v) in waits:
                            eng.wait_ge(k.h, v)
                        if fn is not None:
                            ins = fn(eng)
                            ins.then_inc(sem.h, inc)

                getattr(block, e)(body)


class Arena:
    def __init__(self, t, n):
        self.t, self.n, self.o = t, n, 0

    def reset(self):
        self.o = 0

    def take(self, n):
        assert self.o + n <= self.n, (self.o, n, self.n)
        ap = self.t[:, self.o:self.o + n]
        self.o += n
        return ap


def ssl(start, n, step):
    return slice(start, start + (n - 1) * step + 1, step)


DEBUG = False


def build(NSEG):
    NT = NSEG * 4096
    NS = NT // 2048
    KW = PADK + NT + PADK
    NAg = [NT // (128 * d) + 2 for d in DIL]

    nc = bass.Bass("TRN2", target_bir_lowering=False)
    din = lambda n, s, dt=F32: nc.dram_tensor(n, s, dt, kind="ExternalInput").ap()
    x_d = din("x", [NT, DM])
    w_in_d = din("w_in", [DM, 8704])
    gmix_d = din("g_mix_b", [128, DM])
    gsgu_d = din("g_sgu_b", [128, DM])
    gmlp_d = din("g_mlp_b", [128, DM])
    gfin_d = din("g_fin_b", [128, DM])
    wsT_d = din("w_sT", [128, 8 * 128])
    bs_d = din("b_s", [1, 1024])
    wa_d = din("w_a", [512, DM])
    wb_d = din("w_b", [DM, DM])
    wo_d = din("w_o", [DM, DM])
    wup_d = din("w_up", [DM, 4096])
    wdn_d = din("w_dn", [4096, DM])
    id_d = din("ident", [128, 128])
    tab_d = din("tab", [128, 24 * 256], BF16)
    bnd_d = din("bnd", [128, 24 * 128], BF16)
    y_d = nc.dram_tensor("y", [NT, DM], F32, kind="ExternalOutput").ap()
    scr = lambda n, s, dt: nc.dram_tensor(n, s, dt, kind=("ExternalOutput" if DEBUG else "Internal")).ap()
    hT_s = scr("hT_s", [DM, NT], BF16)
    Q_s = scr("Q_s", [1536, NT], BF16)
    K_s = scr("K_s", [1536, KW], BF16)
    V_s = [scr(f"V_s{g}", [4, DIL[g], 128, NAg[g], 256], BF16) for g in range(3)]
    at_s = scr("at_s", [512, NT], BF16)
    x1_s = scr("x1_s", [NT, DM], F32)

    NB16 = 83456
    NF32 = 11008
    stack = contextlib.ExitStack()
    with stack:
        ab_t = stack.enter_context(nc.sbuf_tensor("arena_b", [128, NB16], BF16))
        af_t = stack.enter_context(nc.sbuf_tensor("arena_f", [128, NF32], F32))
        psb = [stack.enter_context(nc.psum_tensor(f"ps{i}", [128, 512], F32)) for i in range(8)]
        AB = Arena(ab_t, NB16)
        AFa = Arena(af_t, NF32)
        P = Prog(nc, stack)
        PS = [Slot() for _ in range(8)]
        psctr = [0]

        def nbank(lo=0, hi=8):
            i = lo + psctr[0] % (hi - lo)
            psctr[0] += 1
            return i

        ident = AFa.take(128)
        S_const = Slot()
        P.dma("sync", lambda e: e.dma_start(out=ident, in_=id_d), P.sem("dc"), writes=[S_const])
        f32_base = AFa.o

        def cast_load(dst, src, sl, dsem, maxcols=2048):
            ncol = src.shape[-1]
            c0 = 0
            while c0 < ncol:
                c1 = min(ncol, c0 + maxcols)
                P.dma("gpsimd", lambda e, a=dst[:, c0:c1], b=src[:, c0:c1]: e.dma_start(out=a, in_=b),
                      dsem, writes=[sl])
                c0 = c1

        def norm_subtile(st, u, src_rows, gb, S_gb, hT_dst, S_hT, keep_xt=False):
            sl = u % st["nxt"]
            xt, S_xt = st["xt"][sl], st["S_xt"][sl]
            P.dma("sync", lambda e: e.dma_start(out=xt, in_=src_rows), st["Dx"][sl], writes=[S_xt])
            j = u % 8
            ssc, S_ss = st["ss"][:, j:j + 1], st["S_ss"][j]
            sdc = st["sd"][:, j:j + 1]
            rsc = st["rs"][:, j:j + 1]
            P.op("scalar", lambda e: e.activation(out=st["junk"], in_=xt, func=AF.Square, accum_out=ssc),
                 reads=[S_xt], writes=[S_ss, st["S_junk"]])
            P.op("scalar", lambda e: e.activation(out=sdc, in_=ssc, func=AF.Sqrt, bias=st["epsc"], scale=1.0 / DM),
                 reads=[S_ss, S_const], writes=[S_ss])
            hr = P.op("vector", lambda e: e.reciprocal(out=rsc, in_=sdc), reads=[S_ss], writes=[S_ss])
            hs = u % 2
            hf, S_hf = st["hf"][hs], st["S_hf"][hs]
            P.op("vector", lambda e: e.scalar_tensor_tensor(out=hf, in0=xt, scalar=rsc, in1=gb,
                                                            op0=ALU.mult, op1=ALU.mult),
                 reads=[S_xt, S_ss, S_gb], writes=[S_hf], selfdeps=[hr])
            if hT_dst is not None:
                norm_p2(hf, S_hf, hT_dst, S_hT)
            return xt, S_xt, hf, S_hf

        def norm_p2(hf, S_hf, hT_dst, S_hT):
            for half in range(2):
                b = nbank(0, 4)
                for cc in range(4):
                    c = half * 4 + cc
                    P.op("tensor", lambda e, b=b, cc=cc, c=c: e.transpose(psb[b][:, cc * 128:(cc + 1) * 128],
                                                                       hf[:, c * 128:(c + 1) * 128], ident),
                         reads=[S_hf, S_const], writes=[PS[b]])
                dst = hT_dst(half * 4)
                src = psb[b][:, 0:512].rearrange("p (c t) -> p c t", c=4)
                if half == 0:
                    P.op("scalar", lambda e, dst=dst, src=src: e.copy(out=dst, in_=src), reads=[PS[b]], writes=[S_hT])
                else:
                    P.op("vector", lambda e, dst=dst, src=src: e.tensor_copy(out=dst, in_=src), reads=[PS[b]], writes=[S_hT])

        def norm_state(nxt):
            st = {"nxt": nxt}
            st["xt"] = [AFa.take(DM) for _ in range(nxt)]
            st["S_xt"] = [Slot() for _ in range(nxt)]
            st["hf"] = [AFa.take(DM) for _ in range(2)]
            st["S_hf"] = [Slot(), Slot()]
            st["ss"] = AFa.take(8)
            st["sd"] = AFa.take(8)
            st["rs"] = AFa.take(8)
            st["epsc"] = AFa.take(1)
            st["S_ss"] = [Slot() for _ in range(8)]
            st["junk"] = AB.take(DM)
            st["S_junk"] = Slot()
            st["Dx"] = [P.sem("dx") for _ in range(nxt)]
            P.op("vector", lambda e: e.memset(st["epsc"], EPS), writes=[S_const])
            return st

        Dw = P.sem("dw")
        Dht = P.sem("dht")
        Dqk = [P.sem("dqk"), P.sem("dqk")]
        Dvs = [P.sem("dvs"), P.sem("dvs")]
        wq = AB.take(8 * 4608).rearrange("p (k c) -> p k c", k=8)
        S_wq = Slot()
        S_wqk = Slot()
        Dw_qk = P.sem("dw")
        for k in range(8):
            cast_load(wq[:, k, 0:3072], w_in_d[k * 128:(k + 1) * 128, 0:3072], S_wqk, Dw_qk, 1536)
        for k in range(8):
            cast_load(wq[:, k, 3072:4608], w_in_d[k * 128:(k + 1) * 128, 3072:4608], S_wq, Dw, 1536)
        gmix = AFa.take(DM)
        S_g = Slot()
        P.dma("sync", lambda e: e.dma_start(out=gmix, in_=gmix_d), P.sem("dc"), writes=[S_g])
        zt = AB.take(4096)
        S_z = Slot()
        P.op("vector", lambda e: e.memset(zt, 0.0), writes=[S_z])
        stA = norm_state(3)
        hT = AB.take(8 * 2048).rearrange("p (c t) -> p c t", c=8)
        S_hT = Slot()
        qkst = [AB.take(6 * 512).rearrange("p (c t) -> p c t", c=6) for _ in range(2)]
        S_qk = [Slot(), Slot()]
        vst = [AB.take(8 * 1024).rearrange("p (a c) -> p a c", a=8) for _ in range(2)]
        S_vs = [Slot(), Slot()]
        for i_ in range(2):
            P.op("vector", lambda e, i_=i_: e.memset(vst[i_], 1.0), writes=[S_vs[i_]])
        hTv = hT_s.rearrange("(c p) t -> p c t", p=128)
        Qv = Q_s.rearrange("(c p) t -> p c t", p=128)
        qkc = [0]
        vsc = [0]
        evc = [0]

        def evac(dst, src, reads, writes, scale=None):
            evc[0] += 1
            if evc[0] % 2 == 0:
                if scale is None:
                    P.op("scalar", lambda e: e.copy(out=dst, in_=src), reads=reads, writes=writes)
                else:
                    P.op("scalar", lambda e: e.mul(out=dst, in_=src, mul=scale), reads=reads, writes=writes)
            else:
                if scale is None:
                    P.op("vector", lambda e: e.tensor_copy(out=dst, in_=src), reads=reads, writes=writes)
                else:
                    P.op("vector", lambda e: e.tensor_scalar_mul(out=dst, in0=src, scalar1=scale), reads=reads, writes=writes)

        S_hTb = [Slot() for _ in range(4)]

        def a_p1(s, u):
            t0 = s * 2048 + u * 128
            return norm_subtile(stA, s * 16 + u, x_d[t0:t0 + 128, :], gmix, S_g, None, None)

        def a_p2(u, nx):
            norm_p2(nx[2], nx[3], lambda c0, u=u: hT[:, c0:c0 + 4, u * 128:(u + 1) * 128], S_hTb[u // 4])

        def a_norm_group(s, bq):
            us = [4 * bq + i for i in range(4)]
            n0 = a_p1(s, us[0])
            n1 = a_p1(s, us[1])
            a_p2(us[0], n0)
            n2 = a_p1(s, us[2])
            a_p2(us[1], n1)
            n3 = a_p1(s, us[3])
            a_p2(us[2], n2)
            a_p2(us[3], n3)

        def a_vtiles(s, g, r, alist):
            d = DIL[g]
            nq = 16 // d
            n = len(alist)
            sl = vsc[0] % 2
            vsc[0] += 1
            for ai, a in enumerate(alist):
                b = nbank()
                blks = sorted(set(((a * 128 + j) * d + r) // 512 for j in (0, 127)))
                rdb = [S_hTb[x] for x in range(blks[0], blks[-1] + 1)]
                for k in range(8):
                    P.op("tensor", lambda e, b=b, k=k, a=a, r=r, d=d, g=g: e.matmul(
                        psb[b][:, :], lhsT=hT[:, k, ssl(a * 128 * d + r, 128, d)],
                        rhs=wq[:, k, 3072 + g * 512:3072 + (g + 1) * 512], start=(k == 0), stop=(k == 7)),
                        reads=[S_wq] + rdb, writes=[PS[b]])
                pv4 = psb[b][:, :].rearrange("p (q h c) -> p q h c", q=4, h=2)
                vd4 = vst[sl][:, ai, :].rearrange("p (q c) -> p q c", q=4)
                evac(vd4[:, :, 0:64], pv4[:, :, 0, :], [PS[b]], [S_vs[sl]])
                evac(vd4[:, :, 192:256], pv4[:, :, 1, :], [PS[b]], [S_vs[sl]])
            ag = 1 + s * nq + alist[0]
            for q4 in range(4):
                P.dma("gpsimd", lambda e, g=g, r=r, ag=ag, n=n, sl=sl, q4=q4: e.dma_start(
                    out=V_s[g][q4, r, :, ag:ag + n, :],
                    in_=vst[sl][:, 0:n, q4 * 256:(q4 + 1) * 256]), Dvs[sl], reads=[S_vs[sl]])

        def a_norm_steps(s, bq):
            us = [4 * bq + i for i in range(4)]
            st_ = {}

            def mk(i):
                def step():
                    if i == 0:
                        st_[0] = a_p1(s, us[0])
                        st_[1] = a_p1(s, us[1])
                    elif i == 1:
                        a_p2(us[0], st_[0])
                        st_[2] = a_p1(s, us[2])
                    elif i == 2:
                        a_p2(us[1], st_[1])
                        st_[3] = a_p1(s, us[3])
                    elif i == 3:
                        a_p2(us[2], st_[2])
                    else:
                        a_p2(us[3], st_[3])
                return step
            return [mk(i) for i in range(5)]

        def a_qk_block(s, blk, steps):
            T0 = s * 2048
            for grp in range(4):
                sl = qkc[0] % 2
                qkc[0] += 1
                for ci in range(6):
                    ct = grp * 6 + ci
                    b = nbank()
                    for k in range(8):
                        P.op("tensor", lambda e, b=b, k=k, ct=ct, blk=blk: e.matmul(
                            psb[b][:, :], lhsT=wq[:, k, ct * 128:(ct + 1) * 128],
                            rhs=hT[:, k, blk * 512:(blk + 1) * 512], start=(k == 0), stop=(k == 7)),
                            reads=[S_wqk, S_hTb[blk]], writes=[PS[b]])
                    evac(qkst[sl][:, ci, :], psb[b][:, :], [PS[b]], [S_qk[sl]],
                         scale=(0.125 if ct < 12 else None))
                tk = T0 + blk * 512
                if grp < 2:
                    P.dma("gpsimd", lambda e, sl=sl, grp=grp, tk=tk: e.dma_start(
                        out=Qv[:, grp * 6:(grp + 1) * 6, tk:tk + 512], in_=qkst[sl]), Dqk[sl], reads=[S_qk[sl]])
                else:
                    P.dma("gpsimd", lambda e, sl=sl, grp=grp, tk=tk: e.dma_start(
                        out=Kv[:, (grp - 2) * 6:(grp - 1) * 6, PADK + tk:PADK + tk + 512], in_=qkst[sl]),
                        Dqk[sl], reads=[S_qk[sl]])
                if steps:
                    steps.pop(0)()
            a_vtiles(s, 0, 0, [4 * blk + i for i in range(4)])
            while steps:
                steps.pop(0)()

        for bq in range(4):
            for st_ in a_norm_steps(0, bq):
                st_()
        pend_steps = []
        for s in range(NS):
            T0 = s * 2048
            a_qk_block(s, 0, pend_steps)
            P.dma("gpsimd", lambda e, T0=T0: e.dma_start(out=hTv[:, :, T0:T0 + 2048], in_=hT), Dht, reads=S_hTb)
            for r in range(16):
                a_vtiles(s, 2, r, [0])
            for r in range(4):
                a_vtiles(s, 1, r, [0, 1, 2, 3])
            nxt = s + 1 < NS
            a_qk_block(s, 1, a_norm_steps(s + 1, 0) if nxt else [])
            a_qk_block(s, 2, a_norm_steps(s + 1, 1) if nxt else [])
            a_qk_block(s, 3, a_norm_steps(s + 1, 2) if nxt else [])
            pend_steps = a_norm_steps(s + 1, 3) if nxt else []

        Dz = P.sem("dz")
        Kv = K_s.rearrange("(c p) t -> p c t", p=128)
        for c in range(12):
            for side in range(2):
                off = 0 if side == 0 else PADK + NT
                P.dma("sync", lambda e, c=c, off=off: e.dma_start(out=Kv[:, c, off:off + PADK], in_=zt[:, 0:PADK]), Dz, reads=[S_z])
        for g in range(3):
            d = DIL[g]
            for pr in range(4):
                for a in (0, NAg[g] - 1):
                    P.dma("sync", lambda e, g=g, pr=pr, a=a, d=d: e.dma_start(
                        out=V_s[g][pr, :, :, a, :].rearrange("r p c -> p r c"),
                        in_=zt[:, 0:d * 256].rearrange("p (r c) -> p r c", r=d)), Dz, reads=[S_z])


        P.barrier()
        P.new_phase()
        AB.reset()
        AFa.o = f32_base
        Dtab = P.sem("dtab")
        NU = 3
        Dv = [P.sem("dv") for _ in range(NU)]
        Dqk2 = [P.sem("dq") for _ in range(NU)]
        Dast = [P.sem("da"), P.sem("da")]
        tab = AB.take(12 * 512).rearrange("p (h c) -> p h c", h=12)
        bnd = AB.take(12 * 256).rearrange("p (h c) -> p h c", h=12)
        negt = AB.take(128)
        idb = AB.take(128)
        S_tab = Slot()
        P.dma("sync", lambda e: e.dma_start(out=tab, in_=tab_d.rearrange("p (h c) -> p h c", h=12)), Dtab, writes=[S_tab])
        P.dma("sync", lambda e: e.dma_start(out=bnd, in_=bnd_d.rearrange("p (h c) -> p h c", h=12)), Dtab, writes=[S_tab])
        P.op("vector", lambda e: e.memset(negt, NEGV), writes=[S_tab])
        P.op("vector", lambda e: e.tensor_copy(out=idb, in_=ident), reads=[S_const], writes=[S_tab])
        assert NU == 3
        Vw = [AB.take(DIL[g_] * (16 // DIL[g_] + 2) * 256) for g_ in range(NU)]
        S_Vw = [Slot() for _ in range(NU)]
        QAB = [AB.take(4096) for _ in range(NU)]
        S_Q = [Slot() for _ in range(NU)]
        QR = [None] + [AB.take(4096) for _ in range(2)]
        S_QR = [Slot() for _ in range(NU)]
        Kw = [AB.take(2048 + 256 * DIL[g_]) for g_ in range(NU)]
        S_K = [Slot() for _ in range(NU)]
        NPT = 4
        Pt = [AB.take(512) for _ in range(NPT)]
        S_Pt = [Slot() for _ in range(NPT)]
        ast = [AB.take(2048) for _ in range(2)]
        S_ast = [Slot(), Slot()]
        acc = [AFa.take(2 * 2048).rearrange("p (h t) -> p h t", h=2) for _ in range(2)]
        S_acc = [[Slot(), Slot(), Slot()], [Slot(), Slot(), Slot()]]
        rec = AFa.take(2048)
        S_rec = Slot()
        qms = []
        for i in range(NU):
            qms.append(P.op("vector", lambda e, i=i: e.memset(QAB[i], 0.0), writes=[S_Q[i]]))
        unit = [0]
        tilec = [0]
        occ = [0]
        pend = []
        LA = 2
        SB0, OB0 = 2, 5

        def pv_stage(job):
            (slv, r, a, g, d, ptsl, first, asl, after) = job
            b = OB0 + ((occ[0] // 2) % 3)
            j2 = occ[0] % 2
            occ[0] += 1
            Vt = Vw[slv][:, 0:d * (16 // d + 2) * 256].rearrange("p (r a c) -> p r a c", r=d, c=256)
            pt = Pt[ptsl]
            rd = [S_Vw[slv], S_Pt[ptsl]]
            for hd in range(2):
                O = psb[b][:, (j2 * 2 + hd) * 128:(j2 * 2 + hd + 1) * 128]
                hc = slice(hd * 128, (hd + 1) * 128)
                P.op("tensor", lambda e, O=O, hc=hc, hd=hd: e.matmul(O[:, 0:128], lhsT=Vt[:, r, a + 1, hc], rhs=pt[:, hd * 128:(hd + 1) * 128], start=True, stop=False), reads=rd, writes=[PS[b]])
                P.op("tensor", lambda e, O=O, hc=hc, hd=hd: e.matmul(O[:, 0:64], lhsT=Vt[:, r, a, hc], rhs=pt[:, 256 + hd * 64:320 + hd * 64], start=False, stop=False), reads=rd, writes=[PS[b]])
                P.op("tensor", lambda e, O=O, hc=hc, hd=hd: e.matmul(O[:, 64:128], lhsT=Vt[:, r, a + 2, hc], rhs=pt[:, 384 + hd * 64:448 + hd * 64], start=False, stop=True), reads=rd, writes=[PS[b]])
            if j2 != 1:
                assert after is None
                return
            src = psb[b][:, 0:512].rearrange("p (j h q) -> p j h q", j=2, h=2)
            A_ = acc[asl]
            if g == 0:
                dst = A_[:, :, (a - 1) * 128:(a + 1) * 128].rearrange("p h (j q) -> p j h q", j=2)
            elif g == 1:
                dst = A_[:, :, ssl((a - 1) * 512 + r, 256, 4)].rearrange("p h (j q) -> p j h q", j=2)
            else:
                dst = A_.rearrange("p h (q r) -> p r h q", r=16)[:, r - 1:r + 1, :, :]
            if first:
                P.op("vector", lambda e: e.tensor_copy(out=dst, in_=src), reads=[PS[b]], writes=[S_acc[asl][0], S_acc[asl][2]])
            else:
                P.op("vector", lambda e: e.tensor_tensor(out=dst, in0=src, in1=dst, op=ALU.add),
                     reads=[PS[b], S_acc[asl][g - 1]], writes=[S_acc[asl][g]])
            if after is not None:
                after()

        norm_steps = []

        def finish_pair(s, pr, asl):
            A_ = acc[asl]
            SA = S_acc[asl][2]
            for c4 in range(4):
                cs = slice(c4 * 512, (c4 + 1) * 512)
                norm_steps.append(lambda cs=cs: P.op("scalar", lambda e: e.activation(
                    out=rec[0:64, cs], in_=A_[64:128, 0, cs], func=AF.Ln), reads=[SA], writes=[S_rec]))
                norm_steps.append(lambda cs=cs: P.op("scalar", lambda e: e.activation(
                    out=rec[64:128, cs], in_=A_[0:64, 1, cs], func=AF.Ln), reads=[SA], writes=[S_rec]))
                norm_steps.append(lambda cs=cs: P.op("scalar", lambda e: e.activation(
                    out=rec[:, cs], in_=rec[:, cs], func=AF.Exp, scale=-1.0), reads=[S_rec], writes=[S_rec]))

            def fin():
                P.op("gpsimd", lambda e: e.tensor_tensor(
                    out=ast[asl][0:64, :], in0=A_[0:64, 0, :], in1=rec[0:64, :], op=ALU.mult),
                    reads=[SA, S_rec], writes=[S_ast[asl]])
                P.op("gpsimd", lambda e: e.tensor_tensor(
                    out=ast[asl][64:128, :], in0=A_[64:128, 1, :], in1=rec[64:128, :], op=ALU.mult),
                    reads=[SA, S_rec], writes=[S_ast[asl]])
                P.dma("gpsimd", lambda e: e.dma_start(
                    out=at_s[pr * 128:(pr + 1) * 128, s * 2048:(s + 1) * 2048], in_=ast[asl]), Dast[asl], reads=[S_ast[asl]])
            norm_steps.append(fin)

        units = [(s_, pr_, g_) for s_ in range(NS) for pr_ in range(4) for g_ in range(3)]

        def unit_loads(u):
            if u >= len(units):
                return
            s, pr, g = units[u]
            d = DIL[g]
            nq = 16 // d
            sl = g
            nel = d * (nq + 2) * 256
            Vt4 = Vw[sl][:, 0:nel].rearrange("p (r a c) -> p r a c", r=d, c=256)
            P.dma("sync", lambda e: e.dma_start(
                out=Vt4, in_=V_s[g][pr].rearrange("r p a c -> p r a c")[:, :, s * nq:s * nq + nq + 2, :]),
                Dv[sl], writes=[S_Vw[sl]])
            row0 = g * 512 + pr * 128
            P.dma("sync", lambda e: e.dma_start(
                out=QAB[sl][0:64, 0:2048], in_=Q_s[row0:row0 + 64, s * 2048:(s + 1) * 2048]), Dqk2[sl], writes=[S_Q[sl]], deps=qms)
            P.dma("sync", lambda e: e.dma_start(
                out=QAB[sl][64:128, 2048:4096], in_=Q_s[row0 + 64:row0 + 128, s * 2048:(s + 1) * 2048]), Dqk2[sl], writes=[S_Q[sl]], deps=qms)
            kw = 2048 + 256 * d
            k0 = PADK + s * 2048 - 128 * d
            P.dma("sync", lambda e: e.dma_start(
                out=Kw[sl][:, 0:kw], in_=K_s[row0:row0 + 128, k0:k0 + kw]), Dqk2[sl], writes=[S_K[sl]])

        def unit_copy(u):
            if u >= len(units):
                return
            s, pr, g = units[u]
            if g == 0:
                return
            d = DIL[g]
            src = QAB[g].rearrange("p (h m r) -> p h r m", h=2, r=d)
            dst = QR[g].rearrange("p (h r m) -> p h r m", h=2, r=d)
            P.op("scalar", lambda e: e.copy(out=dst[:, 0], in_=src[:, 0]), reads=[S_Q[g]], writes=[S_QR[g]])
            P.op("vector", lambda e: e.tensor_copy(out=dst[:, 1], in_=src[:, 1]), reads=[S_Q[g]], writes=[S_QR[g]])

        unit_loads(0)
        unit_loads(1)
        unit_copy(0)
        for u_, (s, pr, g) in enumerate(units):
            if True:
                asl = (s * 4 + pr) % 2
                if True:
                    d = DIL[g]
                    nq = 16 // d
                    sl = g
                    unit_copy(u_ + 1)
                    jn = 0
                    gh = g * 4 + pr
                    if g == 0:
                        q2 = QAB[sl].rearrange("p (h t) -> p h t", h=2)
                        S_q = S_Q[sl]
                    else:
                        q2r = QR[sl].rearrange("p (h r m) -> p h r m", h=2, r=d)
                        S_q = S_QR[sl]
                    for r in range(d):
                        for a in range(nq):
                            qs = a * 128 * d + r
                            b = SB0 + (tilec[0] % 3)
                            ptsl = tilec[0] % NPT
                            tilec[0] += 1
                            S3 = psb[b]
                            kc = lambda ka, sl=sl, d=d, r=r: Kw[sl][:, ssl(128 * d + ka * 128 * d + r, 128, d)]
                            rd = [S_q, S_K[sl]]
                            if g == 0:
                                qm, ql, qr_ = q2[:, :, ssl(qs, 128, d)], q2[:, :, ssl(qs, 64, d)], q2[:, :, ssl(qs + 64 * d, 64, d)]
                            else:
                                qm = q2r[:, :, r, a * 128:a * 128 + 128]
                                ql = q2r[:, :, r, a * 128:a * 128 + 64]
                                qr_ = q2r[:, :, r, a * 128 + 64:a * 128 + 128]
                            P.op("tensor", lambda e, S3=S3, kc=kc, a=a, qm=qm: e.matmul(
                                S3[:, 0:256], lhsT=kc(a), rhs=qm, start=True, stop=False),
                                reads=rd, writes=[PS[b]])
                            P.op("tensor", lambda e, S3=S3, kc=kc, a=a, ql=ql: e.matmul(
                                S3[:, 256:384], lhsT=kc(a - 1), rhs=ql, start=False, stop=False),
                                reads=rd, writes=[PS[b]])
                            P.op("tensor", lambda e, S3=S3, kc=kc, a=a, qr_=qr_: e.matmul(
                                S3[:, 384:512], lhsT=kc(a + 1), rhs=qr_, start=False, stop=False),
                                reads=rd, writes=[PS[b]])
                            lmode = rmode = 0
                            if a == 0 and s % 2 == 0:
                                lmode = 2 if s == 2 else 1
                            if a == nq - 1 and s % 2 == 1:
                                rmode = 2 if s == 1 else 1
                            if s == NS - 1 and a == nq - 1 and rmode == 2:
                                rmode = 1
                            if lmode == 0 and rmode == 0:
                                P.op("tensor", lambda e, b=b, gh=gh: e.matmul(
                                    psb[b][:, 0:512], lhsT=idb, rhs=tab[:, gh, :], start=False, stop=True),
                                    reads=[S_tab], writes=[PS[b]])
                            else:
                                lsrc = [tab[:, gh, 256:384], negt, bnd[:, gh, 0:128]][lmode]
                                rsrc = [tab[:, gh, 384:512], negt, bnd[:, gh, 128:256]][rmode]
                                P.op("tensor", lambda e, S3=S3, gh=gh: e.matmul(
                                    S3[:, 0:256], lhsT=idb, rhs=tab[:, gh, 0:256], start=False, stop=False),
                                    reads=[S_tab], writes=[PS[b]])
                                P.op("tensor", lambda e, S3=S3, lsrc=lsrc: e.matmul(
                                    S3[:, 256:384], lhsT=idb, rhs=lsrc, start=False, stop=False),
                                    reads=[S_tab], writes=[PS[b]])
                                P.op("tensor", lambda e, S3=S3, rsrc=rsrc: e.matmul(
                                    S3[:, 384:512], lhsT=idb, rhs=rsrc, start=False, stop=True),
                                    reads=[S_tab], writes=[PS[b]])
                            P.op("scalar", lambda e, b=b, ptsl=ptsl: e.activation(out=Pt[ptsl], in_=psb[b][:, 0:512], func=AF.Exp),
                                 reads=[PS[b]], writes=[S_Pt[ptsl]])
                            last = (g == 2 and r == d - 1 and a == nq - 1)
                            pend.append((sl, r, a, g, d, ptsl, g == 0, asl,
                                         (lambda s=s, pr=pr, asl=asl: finish_pair(s, pr, asl)) if last else None))
                            if len(pend) > LA:
                                pv_stage(pend.pop(0))
                            if norm_steps:
                                norm_steps.pop(0)()
                            if jn == LA:
                                unit_loads(u_ + 2)
                            jn += 1
        while pend:
            pv_stage(pend.pop(0))
        while norm_steps:
            norm_steps.pop(0)()

        P.barrier()
        P.new_phase()
        AB.reset()
        AFa.o = f32_base
        Dw2 = P.sem("dw")
        Dlh = [P.sem("dlh"), P.sem("dlh")]
        Dla = [P.sem("dla"), P.sem("dla")]
        Dx2 = [P.sem("dx"), P.sem("dx"), P.sem("dx")]
        Dst3 = [P.sem("dst"), P.sem("dst")]
        S_w2 = Slot()
        wB = AB.take(8 * 4096).rearrange("p (k c) -> p k c", k=8)
        wa = AB.take(4 * 1024).rearrange("p (k c) -> p k c", k=4)
        wb = AB.take(8 * 1024).rearrange("p (k c) -> p k c", k=8)
        wo = AB.take(8 * 1024).rearrange("p (k c) -> p k c", k=8)
        wsT_f = AB.take(1024)
        wsT = wsT_f.rearrange("p (g t) -> p g t", g=8)
        bsr = AB.take(1024)
        onesr = AB.take(128)
        S_wv, S_wu, S_ws, S_wg, S_wo = Slot(), Slot(), Slot(), Slot(), Slot()
        Dwv, Dwu, Dws, Dwg, Dwo = P.sem("dw"), P.sem("dw"), P.sem("dw"), P.sem("dw"), P.sem("dw")
        for k in range(8):
            cast_load(wB[:, k, 1024:2048], w_in_d[k * 128:(k + 1) * 128, 5632:6656], S_wv, Dwv)
        for k in range(8):
            cast_load(wB[:, k, 0:1024], w_in_d[k * 128:(k + 1) * 128, 4608:5632], S_wu, Dwu)
        cast_load(wsT_f, wsT_d, S_ws, Dws)
        P.dma("gpsimd", lambda e: e.dma_start(out=bsr[0:1, :], in_=bs_d), Dws, writes=[S_ws])
        P.op("vector", lambda e: e.memset(onesr, 1.0), writes=[S_ws])
        for k in range(4):
            cast_load(wa[:, k, :], wa_d[k * 128:(k + 1) * 128, :], S_wg, Dwg)
        for k in range(8):
            cast_load(wb[:, k, :], wb_d[k * 128:(k + 1) * 128, :], S_wg, Dwg)
            cast_load(wB[:, k, 2048:4096], w_in_d[k * 128:(k + 1) * 128, 6656:8704], S_wg, Dwg)
        for k in range(8):
            cast_load(wo[:, k, :], wo_d[k * 128:(k + 1) * 128, :], S_wo, Dwo)
        gsgu = AFa.take(DM)
        P.dma("sync", lambda e: e.dma_start(out=gsgu, in_=gsgu_d), P.sem("dc"), writes=[S_g])
        hTt = [AB.take(8 * 512).rearrange("p (c t) -> p c t", c=8) for _ in range(2)]
        S_hTt = [Slot(), Slot()]
        att = [AB.take(4 * 512).rearrange("p (c t) -> p c t", c=4) for _ in range(2)]
        S_att = [Slot(), Slot()]
        uT = AB.take(8 * 512).rearrange("p (c t) -> p c t", c=8)
        S_uT = [Slot() for _ in range(8)]
        vn = AB.take(4 * 1024).rearrange("p (c f) -> p c f", c=4)
        S_vn = Slot()
        mg = AB.take(8 * 512).rearrange("p (c t) -> p c t", c=8)
        S_mg = [Slot() for _ in range(8)]
        junk2 = AB.take(1024)
        S_j2 = Slot()
        G = [AFa.take(1024) for _ in range(2)]
        S_G = [Slot(), Slot()]
        sg = [AFa.take(512) for _ in range(2)]
        S_sg = [Slot(), Slot()]
        tt = [AFa.take(512) for _ in range(2)]
        S_tt = [Slot(), Slot()]
        xc = [AFa.take(1024) for _ in range(3)]
        S_xc = [Slot(), Slot(), Slot()]
        x1c = [AFa.take(1024) for _ in range(2)]
        S_x1 = [Slot(), Slot()]
        st2 = AFa.take(24)
        eps2 = AFa.take(1)
        P.op("vector", lambda e: e.memset(eps2, EPS), writes=[S_const])
        S_st2 = [Slot() for _ in range(8)]
        atv = at_s.rearrange("(c p) t -> p c t", p=128)
        cnt = [0]
        NTL = NT // 512

        def b2_loads(i):
            tk = i * 512
            hs = i % 2
            P.dma("sync", lambda e, hs=hs, tk=tk: e.dma_start(out=hTt[hs], in_=hTv[:, :, tk:tk + 512]), Dlh[hs], writes=[S_hTt[hs]])
            P.dma("sync", lambda e, hs=hs, tk=tk: e.dma_start(out=att[hs], in_=atv[:, :, tk:tk + 512]), Dla[hs], writes=[S_att[hs]])

        def b2_u(i, js):
            tk = i * 512
            hs = i % 2
            hTi, ati = hTt[hs], att[hs]
            for j in js:
                b = nbank()
                for k in range(8):
                    P.op("tensor", lambda e, b=b, k=k, j=j, hTi=hTi: e.matmul(psb[b][:, :], lhsT=wB[:, k, j * 128:(j + 1) * 128],
                                                                      rhs=hTi[:, k, :], start=(k == 0), stop=(k == 7)),
                         reads=[S_wu, S_hTt[hs]], writes=[PS[b]])
                P.op("scalar", lambda e, b=b, j=j: e.activation(out=uT[:, j, :], in_=psb[b][:, :], func=AF.Gelu),
                     reads=[PS[b]], writes=[S_uT[j]])

        def b2_v(i):
            tk = i * 512
            hs = i % 2
            hTi, ati = hTt[hs], att[hs]
            for c in range(4):
                gs = cnt[0] % 2
                jj = cnt[0] % 8
                cnt[0] += 1
                for half in range(2):
                    b = nbank()
                    for k in range(8):
                        P.op("tensor", lambda e, b=b, k=k, c=c, half=half, hTi=hTi: e.matmul(
                            psb[b][:, :], lhsT=hTi[:, k, c * 128:(c + 1) * 128],
                            rhs=wB[:, k, 1024 + half * 512:1024 + (half + 1) * 512], start=(k == 0), stop=(k == 7)),
                            reads=[S_wv, S_hTt[hs]], writes=[PS[b]])
                    P.op("scalar", lambda e, b=b, gs=gs, half=half: e.activation(
                        out=G[gs][:, half * 512:(half + 1) * 512], in_=psb[b][:, :], func=AF.Gelu),
                        reads=[PS[b]], writes=[S_G[gs]])
                ssc, sdc, rsc = st2[:, jj:jj + 1], st2[:, 8 + jj:9 + jj], st2[:, 16 + jj:17 + jj]
                P.op("scalar", lambda e, gs=gs, ssc=ssc: e.activation(out=junk2, in_=G[gs], func=AF.Square, accum_out=ssc),
                     reads=[S_G[gs]], writes=[S_st2[jj], S_j2])
                P.op("scalar", lambda e, ssc=ssc, sdc=sdc: e.activation(out=sdc, in_=ssc, func=AF.Sqrt, bias=eps2, scale=1.0 / DM),
                     reads=[S_const, S_st2[jj]], writes=[S_st2[jj]])
                hr = P.op("vector", lambda e, sdc=sdc, rsc=rsc: e.reciprocal(out=rsc, in_=sdc), reads=[S_st2[jj]], writes=[S_st2[jj]])
                P.op("vector", lambda e, gs=gs, rsc=rsc, c=c: e.scalar_tensor_tensor(
                    out=vn[:, c, :], in0=G[gs], scalar=rsc, in1=gsgu, op0=ALU.mult, op1=ALU.mult),
                    reads=[S_G[gs], S_st2[jj], S_g], writes=[S_vn], selfdeps=[hr])

        def b2_s(i, gis):
            tk = i * 512
            hs = i % 2
            hTi, ati = hTt[hs], att[hs]
            for gi in gis:
                b = nbank()
                for c in range(4):
                    P.op("tensor", lambda e, b=b, c=c, gi=gi: e.matmul(
                        psb[b][:, c * 128:(c + 1) * 128], lhsT=vn[:, c, gi * 128:(gi + 1) * 128], rhs=wsT[:, gi, :],
                        start=True, stop=False), reads=[S_vn, S_ws], writes=[PS[b]])
                    P.op("tensor", lambda e, b=b, c=c, gi=gi: e.matmul(
                        psb[b][:, c * 128:(c + 1) * 128], lhsT=onesr[0:1, :], rhs=bsr[0:1, gi * 128:(gi + 1) * 128],
                        start=False, stop=True), reads=[S_ws], writes=[PS[b]])
                P.op("vector", lambda e, b=b, gi=gi: e.tensor_tensor(out=uT[:, gi, :], in0=psb[b][:, :], in1=uT[:, gi, :], op=ALU.mult),
                     reads=[PS[b], S_uT[gi]], writes=[S_uT[gi]])

        def b2_b(i):
            tk = i * 512
            hs = i % 2
            hTi, ati = hTt[hs], att[hs]
            for j in range(8):
                bA, bB, bGa, bGb = nbank(), nbank(), nbank(), nbank()
                for k in range(4):
                    P.op("tensor", lambda e, k=k, j=j, bA=bA, ati=ati: e.matmul(psb[bA][:, :], lhsT=wa[:, k, j * 128:(j + 1) * 128],
                                                                        rhs=ati[:, k, :], start=(k == 0), stop=(k == 3)),
                         reads=[S_wg, S_att[hs]], writes=[PS[bA]])
                for k in range(8):
                    P.op("tensor", lambda e, k=k, j=j, bB=bB: e.matmul(psb[bB][:, :], lhsT=wb[:, k, j * 128:(j + 1) * 128],
                                                                        rhs=uT[:, k, :], start=(k == 0), stop=(k == 7)),
                         reads=[S_wg, S_uT[k]], writes=[PS[bB]])
                for (bb, off) in ((bGa, 2048), (bGb, 3072)):
                    for k in range(8):
                        P.op("tensor", lambda e, k=k, j=j, bb=bb, off=off, hTi=hTi: e.matmul(
                            psb[bb][:, :], lhsT=wB[:, k, off + j * 128:off + (j + 1) * 128], rhs=hTi[:, k, :],
                            start=(k == 0), stop=(k == 7)), reads=[S_wg, S_hTt[hs]], writes=[PS[bb]])
                P.op("scalar", lambda e, bGa=bGa: e.activation(out=sg[0], in_=psb[bGa][:, :], func=AF.Sigmoid),
                     reads=[PS[bGa]], writes=[S_sg[0]])
                P.op("scalar", lambda e, bGb=bGb: e.activation(out=sg[1], in_=psb[bGb][:, :], func=AF.Sigmoid),
                     reads=[PS[bGb]], writes=[S_sg[1]])
                P.op("vector", lambda e, bA=bA: e.tensor_tensor(out=tt[0], in0=psb[bA][:, :], in1=sg[0], op=ALU.mult),
                     reads=[PS[bA], S_sg[0]], writes=[S_tt[0]])
                P.op("vector", lambda e, bB=bB: e.tensor_tensor(out=tt[1], in0=psb[bB][:, :], in1=sg[1], op=ALU.mult),
                     reads=[PS[bB], S_sg[1]], writes=[S_tt[1]])
                P.op("gpsimd", lambda e, j=j: e.tensor_tensor(out=mg[:, j, :], in0=tt[0], in1=tt[1], op=ALU.add),
                     reads=[S_tt[0], S_tt[1]], writes=[S_mg[j]])

        def b2_o(i):
            tk = i * 512
            hs = i % 2
            hTi, ati = hTt[hs], att[hs]
            for c in range(4):
                xs = (i * 4 + c) % 2
                x3 = (i * 4 + c) % 3
                t0 = tk + c * 128
                P.dma("sync", lambda e, x3=x3, t0=t0: e.dma_start(out=xc[x3], in_=x_d[t0:t0 + 128, :]), Dx2[x3], writes=[S_xc[x3]])
                for half in range(2):
                    b = nbank()
                    for k in range(8):
                        P.op("tensor", lambda e, b=b, k=k, c=c, half=half: e.matmul(
                            psb[b][:, :], lhsT=mg[:, k, c * 128:(c + 1) * 128], rhs=wo[:, k, half * 512:(half + 1) * 512],
                            start=(k == 0), stop=(k == 7)), reads=[S_wo, S_mg[k]], writes=[PS[b]])
                    P.op("vector", lambda e, b=b, xs=xs, x3=x3, half=half: e.tensor_tensor(
                        out=x1c[xs][:, half * 512:(half + 1) * 512], in0=psb[b][:, :], in1=xc[x3][:, half * 512:(half + 1) * 512],
                        op=ALU.add), reads=[PS[b], S_xc[x3]], writes=[S_x1[xs]])
                P.dma("gpsimd", lambda e, xs=xs, t0=t0: e.dma_start(out=x1_s[t0:t0 + 128, :], in_=x1c[xs]), Dst3[xs], reads=[S_x1[xs]])


        b2_loads(0)
        b2_v(0)
        for i in range(NTL):
            b2_u(i, [0, 1])
            for j_ in range(2, 8):
                b2_s(i, [j_ - 2])
                b2_u(i, [j_])
            b2_s(i, [6, 7])
            b2_b(i)
            if i + 1 < NTL:
                b2_loads(i + 1)
                b2_v(i + 1)
            b2_o(i)

        P.barrier()
        P.new_phase()
        AB.reset()
        AFa.o = f32_base
        Dw3 = P.sem("dw")
        Dst4 = [P.sem("dst"), P.sem("dst")]
        S_w3 = Slot()
        wup = AB.take(8 * 4096).rearrange("p (k c) -> p k c", k=8)
        wdn = AB.take(32 * 1024).rearrange("p (k c) -> p k c", k=32)
        S_wdn = Slot()
        Dwdn = P.sem("dw")
        for k in range(8):
            cast_load(wup[:, k, :], wup_d[k * 128:(k + 1) * 128, :], S_w3, Dw3)
        for k in range(32):
            cast_load(wdn[:, k, :], wdn_d[k * 128:(k + 1) * 128, :], S_wdn, Dwdn)
        gmlp = AFa.take(DM)
        gfin = AFa.take(DM)
        P.dma("sync", lambda e: e.dma_start(out=gmlp, in_=gmlp_d), P.sem("dc"), writes=[S_g])
        P.dma("sync", lambda e: e.dma_start(out=gfin, in_=gfin_d), P.sem("dc"), writes=[S_g])
        stC = norm_state(4)
        h2T = [AB.take(8 * 256).rearrange("p (c t) -> p c t", c=8) for _ in range(2)]
        S_h2 = [Slot(), Slot()]
        upT = AB.take(32 * 256).rearrange("p (c t) -> p c t", c=32)
        S_up = [Slot() for _ in range(32)]
        sq = [AFa.take(256) for _ in range(2)]
        S_sq = [Slot(), Slot()]
        x2 = [AFa.take(1024) for _ in range(2)]
        S_x2 = [Slot(), Slot()]
        st3 = AFa.take(24)
        S_st3 = [Slot() for _ in range(8)]
        junk3 = AB.take(1024)
        S_j3 = Slot()
        NTC = NT // 256
        cc2 = [0]
        def c_p1(i):
            return [norm_subtile(stC, i * 2 + c, x1_s[i * 256 + c * 128:i * 256 + (c + 1) * 128, :], gmlp, S_g, None, None)
                    for c in range(2)]

        def c_p2(i, nx):
            for c in range(2):
                norm_p2(nx[c][2], nx[c][3],
                        lambda c0, c=c, i=i: h2T[i % 2][:, c0:c0 + 4, c * 128:(c + 1) * 128], S_h2[i % 2])

        nxt_x = c_p1(0)
        c_p2(0, nxt_x)
        for i in range(NTC):
            hs = i % 2
            xts = nxt_x
            for j in range(32):
                if j == 16 and i + 1 < NTC:
                    nxt_x = c_p1(i + 1)
                b = nbank()
                for k in range(8):
                    P.op("tensor", lambda e, b=b, k=k, j=j, hs=hs: e.matmul(
                        psb[b][:, 0:256], lhsT=wup[:, k, j * 128:(j + 1) * 128], rhs=h2T[hs][:, k, :],
                        start=(k == 0), stop=(k == 7)), reads=[S_w3, S_h2[hs]], writes=[PS[b]])
                qs_ = j % 2
                P.op("scalar", lambda e, b=b, qs_=qs_: e.activation(out=sq[qs_], in_=psb[b][:, 0:256], func=AF.Square),
                     reads=[PS[b]], writes=[S_sq[qs_]])
                P.op("vector", lambda e, b=b, qs_=qs_, j=j: e.scalar_tensor_tensor(
                    out=upT[:, j, :], in0=psb[b][:, 0:256], scalar=0.0, in1=sq[qs_], op0=ALU.is_gt, op1=ALU.mult),
                    reads=[PS[b], S_sq[qs_]], writes=[S_up[j]])
            for c in range(2):
                if c == 1 and i + 1 < NTC:
                    c_p2(i + 1, nxt_x)
                t0 = i * 256 + c * 128
                xt, S_xt = xts[c][0], xts[c][1]
                xs = cc2[0] % 2
                jj = cc2[0] % 8
                cc2[0] += 1
                for half in range(2):
                    b = nbank()
                    for k in range(32):
                        P.op("tensor", lambda e, b=b, k=k, c=c, half=half: e.matmul(
                            psb[b][:, :], lhsT=upT[:, k, c * 128:(c + 1) * 128], rhs=wdn[:, k, half * 512:(half + 1) * 512],
                            start=(k == 0), stop=(k == 31)), reads=[S_wdn, S_up[k]], writes=[PS[b]])
                    P.op("vector", lambda e, b=b, xs=xs, half=half, xt=xt: e.tensor_tensor(
                        out=x2[xs][:, half * 512:(half + 1) * 512], in0=psb[b][:, :], in1=xt[:, half * 512:(half + 1) * 512],
                        op=ALU.add), reads=[PS[b], S_xt], writes=[S_x2[xs]])
                ssc, sdc, rsc = st3[:, jj:jj + 1], st3[:, 8 + jj:9 + jj], st3[:, 16 + jj:17 + jj]
                P.op("scalar", lambda e, xs=xs, ssc=ssc: e.activation(out=junk3, in_=x2[xs], func=AF.Square, accum_out=ssc),
                     reads=[S_x2[xs]], writes=[S_st3[jj], S_j3])
                P.op("scalar", lambda e, ssc=ssc, sdc=sdc: e.activation(out=sdc, in_=ssc, func=AF.Sqrt, bias=stC["epsc"], scale=1.0 / DM),
                     reads=[S_const, S_st3[jj]], writes=[S_st3[jj]])
                hr = P.op("vector", lambda e, sdc=sdc, rsc=rsc: e.reciprocal(out=rsc, in_=sdc), reads=[S_st3[jj]], writes=[S_st3[jj]])
                P.op("vector", lambda e, xs=xs, rsc=rsc: e.scalar_tensor_tensor(
                    out=x2[xs], in0=x2[xs], scalar=rsc, in1=gfin, op0=ALU.mult, op1=ALU.mult),
                    reads=[S_st3[jj], S_g, S_x2[xs]], writes=[S_x2[xs]], selfdeps=[hr])
                P.dma("gpsimd", lambda e, xs=xs, t0=t0: e.dma_start(out=y_d[t0:t0 + 128, :], in_=x2[xs]), Dst4[xs], reads=[S_x2[xs]])

        P.barrier()
        P.emit()
    return nc


def _tables():
    slopes = (2.0 ** (-8.0 * np.arange(1, 25) / 24)).astype(np.float32)
    k = np.arange(128)[:, None].astype(np.float32)
    tab = np.full((128, 12, 512), NEGV, np.float32)
    for g in range(3):
        for pr in range(4):
            for hd in range(2):
                c = slopes[g * 8 + pr * 2 + hd] * DIL[g]
                q = np.arange(128)[None, :].astype(np.float32)
                rel = np.abs(k - q)
                tab[:, g * 4 + pr, hd * 128:(hd + 1) * 128] = np.where(rel <= 64, -c * rel, NEGV)
                q = np.arange(64)[None, :].astype(np.float32)
                dist = 128 + q - k
                tab[:, g * 4 + pr, 256 + hd * 64:320 + hd * 64] = np.where(dist <= 64, -c * dist, NEGV)
                dist = 64 + k - q
                tab[:, g * 4 + pr, 384 + hd * 64:448 + hd * 64] = np.where(dist <= 64, -c * dist, NEGV)
    return tab


_CACHE = {}


def _get_nc(nseg):
    if nseg not in _CACHE:
        _CACHE[nseg] = build(nseg)
    return _CACHE[nseg]


def run_cores(xs, cont, w, nseg):
    tab = _tables()
    tabb = tab.reshape(128, 12 * 512).astype(ml_dtypes.bfloat16)
    bnd_c = np.ascontiguousarray(tab[:, :, 256:512]).reshape(128, 12 * 256).astype(ml_dtypes.bfloat16)
    bnd_m = np.full((128, 12 * 256), NEGV, np.float32).astype(ml_dtypes.bfloat16)
    rep = lambda v: np.ascontiguousarray(np.broadcast_to(np.asarray(v, np.float32).reshape(1, DM), (128, DM)))
    common = {
        "w_in": np.ascontiguousarray(w["w_in"][0]),
        "g_mix_b": rep(w["g_mix"][0]), "g_sgu_b": rep(w["g_sgu"][0]),
        "g_mlp_b": rep(w["g_mlp"][0]), "g_fin_b": rep(w["g_final"]),
        "w_sT": np.ascontiguousarray(np.transpose(w["w_s"][0], (2, 0, 1)).reshape(128, 1024)),
        "b_s": np.ascontiguousarray(w["b_s"][0].reshape(1, 1024)),
        "w_a": np.ascontiguousarray(w["w_branch_a"][0]), "w_b": np.ascontiguousarray(w["w_branch_b"][0]),
        "w_o": np.ascontiguousarray(w["w_out"][0]),
        "w_up": np.ascontiguousarray(w["w_up"][0]), "w_dn": np.ascontiguousarray(w["w_down"][0]),
        "ident": np.eye(128, dtype=np.float32), "tab": tabb,
    }
    in_maps = []
    for x, c in zip(xs, cont):
        m = dict(common)
        m["x"] = np.ascontiguousarray(x)
        m["bnd"] = bnd_c if c else bnd_m
        in_maps.append(m)
    nc = _get_nc(nseg)
    res = run_bass_kernel_spmd(nc, in_maps, core_ids=list(range(len(xs))))
    if DEBUG:
        return res.results
    return [r["y"] for r in res.results]


def kernel(x_prompt, x_sample, g_mix, w_in, w_s, b_s, g_sgu, w_branch_a, w_branch_b, w_out,
           g_mlp, w_up, w_down, g_final):
    w = dict(g_mix=np.asarray(g_mix), w_in=np.asarray(w_in), w_s=np.asarray(w_s), b_s=np.asarray(b_s),
             g_sgu=np.asarray(g_sgu), w_branch_a=np.asarray(w_branch_a), w_branch_b=np.asarray(w_branch_b),
             w_out=np.asarray(w_out), g_mlp=np.asarray(g_mlp), w_up=np.asarray(w_up), w_down=np.asarray(w_down),
             g_final=np.asarray(g_final))
    xp = np.asarray(x_prompt, np.float32)
    xs_ = np.asarray(x_sample, np.float32)
    xs, cont = [], []
    for c in range(4):
        xs.append(np.concatenate([xs_[c], xp[c]], axis=0))
        cont.append(True)
    for c in range(4):
        xs.append(np.concatenate([xp[4 + 3 * c], xp[5 + 3 * c], xp[6 + 3 * c]], axis=0))
        cont.append(False)
    ys = run_cores(xs, cont, w, 3)
    y_p = np.empty_like(xp)
    y_s = np.empty_like(xs_)
    for c in range(4):
        y_s[c] = ys[c][0:8192]
        y_p[c] = ys[c][8192:12288]
    for c in range(4):
        for j in range(3):
            y_p[4 + 3 * c + j] = ys[4 + c][j * 4096:(j + 1) * 4096]
    return (y_p, y_s)
```
